# Optimizing a Trainium2 kernel written in Bass

```python
import jax, jax.numpy as jnp
from jax import lax
import numpy as np


D_MODEL = 1024
BATCH = 8
SEQ = 2048
DEPTH = 2

MIX = D_MODEL
HEAD_DIM = 64
ATT_WIDTH = MIX // 2
N_Q_HEADS = ATT_WIDTH // HEAD_DIM
N_KV_HEADS = N_Q_HEADS // 4
GQA_GROUP = N_Q_HEADS // N_KV_HEADS
WINDOW = 128
BLOCK = 128
CV_CH = MIX // 4
CONV_K = 31
RET_WIDTH = MIX // 4
RET_DK = 64
RET_DV = 64
RET_HEADS = RET_WIDTH // RET_DV
N_META = 16
PAD = BLOCK - N_META
D_FF = ((8 * D_MODEL // 3 + 127) // 128) * 128
FFN_CONV_K = 3
RMS_EPS = 1e-6
LN_EPS = 1e-5
SPLIT_SIZES = (N_Q_HEADS * HEAD_DIM, N_KV_HEADS * HEAD_DIM, N_KV_HEADS * HEAD_DIM,
               CV_CH, CV_CH,
               RET_HEADS * RET_DK, RET_HEADS * RET_DK, RET_HEADS * RET_DV, RET_WIDTH)
IN_WIDTH = sum(SPLIT_SIZES)

kernel_name = 'hymba_swa_conformer_retention_convffn'


def rmsnorm(x, g):
    xf = x.astype(jnp.float32)
    y = xf * lax.rsqrt(jnp.mean(xf * xf, axis=-1, keepdims=True) + RMS_EPS)
    return (y * g.astype(jnp.float32)).astype(x.dtype)


def layernorm(x, g, b):
    xf = x.astype(jnp.float32)
    mu = jnp.mean(xf, axis=-1, keepdims=True)
    var = jnp.mean(jnp.square(xf - mu), axis=-1, keepdims=True)
    y = (xf - mu) * lax.rsqrt(var + LN_EPS)
    return (y * g.astype(jnp.float32) + b.astype(jnp.float32)).astype(x.dtype)


def causal_dwconv(x, w, b):
    ksz, ch = w.shape
    y = lax.conv_general_dilated(
        x, w.astype(x.dtype)[:, None, :], window_strides=(1,), padding=[(ksz - 1, 0)],
        dimension_numbers=('NWC', 'WIO', 'NWC'), feature_group_count=ch)
    return y + b.astype(x.dtype)


def split_cols(z):
    out, start = [], 0
    for w in SPLIT_SIZES:
        out.append(z[..., start:start + w])
        start += w
    return out


def sliding_window_sink_attention(q, k, v, sinks, slopes):
    B, L = q.shape[0], q.shape[1]
    nb = L // BLOCK
    qb = q.reshape(B, nb, BLOCK, N_KV_HEADS, GQA_GROUP, HEAD_DIM)

    def band(z):
        zb = z.reshape(B, nb, BLOCK, N_KV_HEADS, HEAD_DIM)
        prev = jnp.concatenate([jnp.zeros_like(zb[:, :1]), zb[:, :-1]], axis=1)
        return jnp.concatenate([prev, zb], axis=2)

    def meta(z):
        return jnp.broadcast_to(z[:, None, PAD:BLOCK], (B, nb, N_META, N_KV_HEADS, HEAD_DIM))

    keys = jnp.concatenate([meta(k), band(k)], axis=2)
    vals = jnp.concatenate([meta(v), band(v)], axis=2)

    blk = jnp.arange(nb)[:, None]
    t = blk * BLOCK + jnp.arange(BLOCK)[None, :]
    s_band = (blk - 1) * BLOCK + jnp.arange(2 * BLOCK)[None, :]
    s_meta = PAD + jnp.arange(N_META)
    d_band = t[:, :, None] - s_band[:, None, :]
    d_meta = t[:, :, None] - s_meta[None, None, :]
    ok_band = (d_band >= 0) & (d_band < WINDOW) & (s_band[:, None, :] >= BLOCK)
    ok_meta = d_meta >= 0
    ok = jnp.concatenate([ok_meta, ok_band], axis=-1)
    dist = jnp.concatenate([jnp.minimum(d_meta, WINDOW), d_band], axis=-1).astype(jnp.float32)

    scale = HEAD_DIM ** -0.5
    logits = jnp.einsum('bnihgd,bnjhd->bnhgij', qb, keys).astype(jnp.float32) * scale
    sl = slopes.reshape(N_KV_HEADS, GQA_GROUP)[None, None, :, :, None, None]
    logits = logits - sl * dist[None, :, None, None, :, :]
    logits = jnp.where(ok[None, :, None, None], logits, -jnp.inf)
    sink = jnp.broadcast_to(
        sinks.astype(jnp.float32).reshape(N_KV_HEADS, GQA_GROUP)[None, None, :, :, None, None],
        logits.shape[:-1] + (1,))
    probs = jax.nn.softmax(jnp.concatenate([logits, sink], axis=-1), axis=-1)[..., :-1]
    out = jnp.einsum('bnhgij,bnjhd->bnihgd', probs.astype(v.dtype), vals)
    return out.reshape(B, L, N_Q_HEADS * HEAD_DIM)


def conformer_conv(a, b, w_dw, b_dw, ln_g, ln_b, w_pw):
    u = a * jax.nn.sigmoid(b)
    u = causal_dwconv(u, w_dw, b_dw)
    u = layernorm(u, ln_g, ln_b)
    u = jax.nn.silu(u)
    return u @ w_pw.astype(u.dtype)


def chunkwise_retention(q, k, v, g, gn_g, valid):
    B, L = q.shape[0], q.shape[1]
    nc = L // BLOCK
    qf = q.astype(jnp.float32).reshape(B, nc, BLOCK, RET_HEADS, RET_DK)
    kf = (k.astype(jnp.float32) * (RET_DK ** -0.5) * valid[:, :, None, None].astype(jnp.float32))
    kf = kf.reshape(B, nc, BLOCK, RET_HEADS, RET_DK)
    vf = v.astype(jnp.float32).reshape(B, nc, BLOCK, RET_HEADS, RET_DV)

    log_gamma = jnp.log1p(-jnp.exp2(-5.0 - jnp.arange(RET_HEADS, dtype=jnp.float32)))
    idx = jnp.arange(BLOCK, dtype=jnp.float32)
    diff = idx[:, None] - idx[None, :]
    decay = jnp.where(diff[None] >= 0, jnp.exp(jnp.maximum(diff, 0.0)[None] * log_gamma[:, None, None]), 0.0)
    zeta = jnp.exp((BLOCK - 1 - idx)[None, :] * log_gamma[:, None])
    xi = jnp.exp((idx + 1.0)[None, :] * log_gamma[:, None])
    chunk_decay = jnp.exp(BLOCK * log_gamma)

    scores = jnp.einsum('bnihd,bnjhd->bnhij', qf, kf) * decay[None, None]
    y = jnp.einsum('bnhij,bnjhe->bnihe', scores, vf)
    kv = jnp.einsum('bnjhd,bnjhe,hj->bnhde', kf, vf, zeta)

    def step(state, kv_n):
        return chunk_decay[None, :, None, None] * state + kv_n, state

    init = jnp.zeros((B, RET_HEADS, RET_DK, RET_DV), jnp.float32)
    _, r_prev = lax.scan(step, init, jnp.moveaxis(kv, 1, 0))
    r_prev = jnp.moveaxis(r_prev, 0, 1)
    y = y + jnp.einsum('bnihd,bnhde,hi->bnihe', qf, r_prev, xi)

    y = y.reshape(B, L, RET_HEADS, RET_DV)
    mu = jnp.mean(y, axis=-1, keepdims=True)
    var = jnp.mean(jnp.square(y - mu), axis=-1, keepdims=True)
    y = ((y - mu) * lax.rsqrt(var + LN_EPS)).reshape(B, L, RET_WIDTH) * gn_g.astype(jnp.float32)
    return jax.nn.silu(g) * y.astype(g.dtype)


def setup_inputs(seed: int = 0) -> dict:
    key = jax.random.key(seed)
    ks = jax.random.split(key, 22)
    f32 = jnp.float32

    def nrm(k, shape, scale):
        return jax.random.normal(k, shape, f32) * scale

    return {
        'x': nrm(ks[0], (BATCH, SEQ, D_MODEL), 1.0),
        'meta': nrm(ks[1], (N_META, D_MODEL), 1.0),
        'norm_mix_g': 1.0 + nrm(ks[2], (DEPTH, D_MODEL), 0.02),
        'w_in': nrm(ks[3], (DEPTH, D_MODEL, IN_WIDTH), D_MODEL ** -0.5),
        'q_norm_g': 1.0 + nrm(ks[4], (DEPTH, HEAD_DIM), 0.02),
        'k_norm_g': 1.0 + nrm(ks[5], (DEPTH, HEAD_DIM), 0.02),
        'attn_sinks': nrm(ks[6], (DEPTH, N_Q_HEADS), 0.5),
        'attn_out_g': 1.0 + nrm(ks[7], (DEPTH, ATT_WIDTH), 0.02),
        'cv_dw_w': nrm(ks[8], (DEPTH, CONV_K, CV_CH), CONV_K ** -0.5),
        'cv_dw_b': nrm(ks[9], (DEPTH, CV_CH), 0.02),
        'cv_ln_g': 1.0 + nrm(ks[10], (DEPTH, CV_CH), 0.02),
        'cv_ln_b': nrm(ks[11], (DEPTH, CV_CH), 0.02),
        'cv_pw': nrm(ks[12], (DEPTH, CV_CH, CV_CH), CV_CH ** -0.5),
        'cv_out_g': 1.0 + nrm(ks[13], (DEPTH, CV_CH), 0.02),
        'ret_gn_g': 1.0 + nrm(ks[14], (DEPTH, RET_WIDTH), 0.02),
        'w_out': nrm(ks[15], (DEPTH, MIX, D_MODEL), MIX ** -0.5),
        'norm_ffn_g': 1.0 + nrm(ks[16], (DEPTH, D_MODEL), 0.02),
        'ffn_up': nrm(ks[17], (DEPTH, D_MODEL, 2 * D_FF), D_MODEL ** -0.5),
        'ffn_dw_w': nrm(ks[18], (DEPTH, FFN_CONV_K, 2 * D_FF), FFN_CONV_K ** -0.5),
        'ffn_dw_b': nrm(ks[19], (DEPTH, 2 * D_FF), 0.02),
        'ffn_down': nrm(ks[20], (DEPTH, D_FF, D_MODEL), D_FF ** -0.5),
    }


def reference(x, meta, norm_mix_g, w_in, q_norm_g, k_norm_g, attn_sinks, attn_out_g,
              cv_dw_w, cv_dw_b, cv_ln_g, cv_ln_b, cv_pw, cv_out_g, ret_gn_g, w_out,
              norm_ffn_g, ffn_up, ffn_dw_w, ffn_dw_b, ffn_down):
    B, S, D = x.shape
    L = S + BLOCK
    dt = x.dtype
    h = jnp.concatenate([
        jnp.zeros((B, PAD, D), dt),
        jnp.broadcast_to(meta.astype(dt)[None], (B, N_META, D)),
        x], axis=1)
    valid = (jnp.arange(L) >= PAD).astype(dt)[None, :, None]
    slopes = jnp.exp2(-8.0 * jnp.arange(1, N_Q_HEADS + 1, dtype=jnp.float32) / N_Q_HEADS)

    for l in range(DEPTH):
        u = rmsnorm(h, norm_mix_g[l])
        proj = u @ w_in[l].astype(dt)
        q, k, v, ca, cb, rq, rk, rv, rg = split_cols(proj)
        q = rmsnorm(q.reshape(B, L, N_Q_HEADS, HEAD_DIM), q_norm_g[l])
        k = rmsnorm(k.reshape(B, L, N_KV_HEADS, HEAD_DIM), k_norm_g[l])
        v = v.reshape(B, L, N_KV_HEADS, HEAD_DIM)
        y_att = rmsnorm(sliding_window_sink_attention(q, k, v, attn_sinks[l], slopes), attn_out_g[l])
        y_cv = rmsnorm(conformer_conv(ca, cb, cv_dw_w[l], cv_dw_b[l], cv_ln_g[l], cv_ln_b[l], cv_pw[l]),
                       cv_out_g[l])
        y_ret = chunkwise_retention(rq.reshape(B, L, RET_HEADS, RET_DK),
                                    rk.reshape(B, L, RET_HEADS, RET_DK),
                                    rv.reshape(B, L, RET_HEADS, RET_DV),
                                    rg, ret_gn_g[l], valid[..., 0])
        y = jnp.concatenate([y_att, y_cv, y_ret], axis=-1) @ w_out[l].astype(dt)
        h = h + valid * y
        f = rmsnorm(h, norm_ffn_g[l]) @ ffn_up[l].astype(dt)
        f = causal_dwconv(f, ffn_dw_w[l], ffn_dw_b[l])
        fg, fu = f[..., :D_FF], f[..., D_FF:]
        h = h + valid * ((jax.nn.silu(fg) * fu) @ ffn_down[l].astype(dt))

    return h[:, BLOCK:, :]
```

```python
import numpy as np
import concourse.bass as bass
import concourse.mybir as mybir
from concourse.bass_utils import run_bass_kernel_spmd

F32 = mybir.dt.float32
BF16 = mybir.dt.bfloat16
AF = mybir.ActivationFunctionType
ALU = mybir.AluOpType

D = 1024
S = 2048
T = 2176
NT = 17
DEPTH = 2
DFF = 2816
NCF = 22
IN_W = 2304
GRP = [(0, 128), (128, 512), (640, 512), (1152, 512), (1664, 512)]
GRP16 = [(112, 16)] + GRP[1:]
PARTS = [list(range(0, 8)), list(range(8, 15)), list(range(15, 22))]
NB = 6
NSF = 11
NSB = 12
BIG = 1.0e9
SLOPES = [float(2.0 ** (-8.0 * (i + 1) / 8.0)) for i in range(8)]
RMS_EPS = 1e-6
HPOS = {0: 0, 2: 1, 1: 2, 3: 3}
LN_EPS = 1e-5


class Op:
    __slots__ = ("eng", "fn", "reads", "writes", "dma", "dkey", "ndma", "deps",
                 "signal", "sem", "val", "vc", "idx")


class Prog:
    def __init__(self, nc):
        self.nc = nc
        self.ops = []
        self.last_w = {}
        self.readers = {}
        self.last_acc = {}
        self.engs = {"pe": nc.tensor, "dve": nc.vector, "act": nc.scalar,
                     "pool": nc.gpsimd, "sp": nc.sync}

    def add(self, eng, fn, reads=(), writes=(), dma=False, dkey=None, ndma=1):
        o = Op()
        o.eng, o.fn, o.dma, o.dkey, o.ndma = eng, fn, dma, dkey, ndma
        o.reads, o.writes = tuple(reads), tuple(writes)
        o.idx = len(self.ops)
        o.signal = False
        o.vc = None
        deps = {}
        for k in o.reads:
            w = self.last_w.get(k)
            if w is not None:
                deps[(w.idx, "raw")] = w
        for k in o.writes:
            w = self.last_w.get(k)
            if w is not None:
                deps[(w.idx, "waw")] = w
            for r in self.readers.get(k, ()):
                deps[(r.idx, "war")] = r
        for k in o.reads + o.writes:
            if isinstance(k, tuple) and k[0] == "ps":
                la = self.last_acc.setdefault(k, {})
                for e2, d in la.items():
                    if e2 != eng:
                        deps[(d.idx, "x")] = d
                la[eng] = o
        real = {}
        for (di, kind), d in deps.items():
            if d is o:
                continue
            if (not d.dma) and (not o.dma) and d.eng == o.eng:
                if o.eng == "pe":
                    continue
            real[di] = d
        o.deps = list(real.values())
        for d in o.deps:
            d.signal = True
        for k in o.reads:
            self.readers.setdefault(k, []).append(o)
        for k in o.writes:
            self.last_w[k] = o
            self.readers[k] = []
        self.ops.append(o)
        return o

    def emit(self, final_wait_ops=()):
        nc = self.nc
        esem = {e: nc.alloc_semaphore("s_" + e) for e in self.engs}
        ecount = {e: 0 for e in self.engs}
        dsem, dcount = {}, {}
        for o in final_wait_ops:
            o.signal = True
        for o in self.ops:
            if o.dma:
                if o.dkey not in dsem:
                    dsem[o.dkey] = nc.alloc_semaphore("d_%d" % (len(dsem),))
                    dcount[o.dkey] = 0
                dcount[o.dkey] += 16 * o.ndma
                o.sem, o.val = dsem[o.dkey], dcount[o.dkey]
            elif o.signal:
                ecount[o.eng] += 1
                o.sem, o.val = esem[o.eng], ecount[o.eng]
            else:
                o.sem, o.val = None, None
        known = {e: {} for e in self.engs}
        nwait = 0
        for o in self.ops:
            kn = known[o.eng]
            eng = self.engs[o.eng]
            need = {}
            for d in o.deps:
                sid = id(d.sem)
                if kn.get(sid, 0) >= d.val:
                    continue
                if sid not in need or need[sid][1] < d.val:
                    need[sid] = (d.sem, d.val, d)
            for sid, (sem, val, d) in sorted(need.items(), key=lambda t: -t[1][2].idx):
                if kn.get(sid, 0) >= val:
                    continue
                eng.wait_ge(sem, val)
                nwait += 1
                for s2, v2 in d.vc.items():
                    if kn.get(s2, 0) < v2:
                        kn[s2] = v2
            res = o.fn(eng)
            if o.dma:
                insts = res if isinstance(res, (list, tuple)) else [res]
                assert len(insts) == o.ndma, (len(insts), o.ndma)
                for i in insts:
                    i.then_inc(o.sem, 16)
                o.vc = dict(kn)
                o.vc[id(o.sem)] = o.val
            elif o.signal:
                res.then_inc(o.sem, 1)
                o.vc = dict(kn)
                o.vc[id(o.sem)] = o.val
        for o in final_wait_ops:
            self.engs["sp"].wait_ge(o.sem, o.val)
        for k, sem in dsem.items():
            self.engs["sp"].wait_ge(sem, dcount[k])
        self.nwait = nwait


def perm_att():
    idx = np.zeros((4, 128), np.int64)
    for c in range(4):
        for p in range(128):
            idx[c, p] = (4 * (p // 64) + c) * 64 + p % 64
    return idx


def vec_layout():
    names = [("mix_g", 8), ("ffn_g", 8), ("qg", 1), ("kg", 1), ("aog", 4), ("cvw", 62),
             ("cvb", 2), ("lng", 2), ("lnb", 2), ("cog", 2), ("gng", 2), ("fw", 132), ("fb", 44)]
    off, o = {}, 0
    for n, c in names:
        off[n] = o
        o += c
    return off, o


VOFF, NV = vec_layout()
A_A, A_MC, A_MP, A_OM, NALB = 1536, 1664, 1792, 1920, 1984
C_ATTD, C_DEC2, C_XI, C_ZETA, C_CDEC, C_ID, NCST = 0, 640, 1152, 1408, 1664, 1666, 1794


def unit_schedule():
    sch = []
    for nm in ("v", "rv0", "rv1", "rk0", "rk1"):
        sch.append(("tm", nm))
    for j in range(2):
        sch.append(("in", "ca%d" % j))
        sch.append(("in", "cb%d" % j))
    sch.append(("pw",))
    sch.append(("wo_cv", 0))
    sch.append(("wo_cv", 1))
    for nm in ("rq0", "rq1", "rkf0", "rkf1"):
        sch.append(("in", nm))
    sch.append(("in", "rg0"))
    sch.append(("in", "rg1"))
    sch.append(("wo_ret", 0))
    sch.append(("wo_ret", 1))
    for nm in ("q0", "q1", "q2", "q3", "k"):
        sch.append(("in", nm))
    for a in range(4):
        sch.append(("wo_att", a))
    for pi, part in enumerate(PARTS):
        for c in part:
            sch.append(("up", c, 0))
            sch.append(("up", c, 1))
        for m in range(8):
            sch.append(("down", pi, m))
    return sch


def in_cols(nm):
    pa = perm_att()
    base = {"k": 512, "v": 640, "ca": 768, "cb": 1024, "rq": 1280, "rk": 1536, "rkf": 1536,
            "rv": 1792, "rg": 2048}
    if nm[0] == "q" and len(nm) == 2:
        return pa[int(nm[1])]
    if nm in ("k", "v"):
        return base[nm] + np.arange(128)
    b = nm[:-1]
    return base[b] + int(nm[-1]) * 128 + np.arange(128)


def kc_unit(W, cols):
    return W[:, cols].reshape(8, 128, len(cols)).transpose(1, 0, 2).reshape(128, -1)


def pack_units(w_in, cv_pw, w_out, ffn_up, ffn_down):
    pa = perm_att()
    sch = unit_schedule()
    out = np.zeros((DEPTH * len(sch), 128, 1024), np.float32)
    for l in range(DEPTH):
        for ui, u in enumerate(sch):
            dst = out[l * len(sch) + ui]
            if u[0] in ("tm", "in"):
                dst[:, :] = kc_unit(w_in[l], in_cols(u[1]))
            elif u[0] == "pw":
                dst[:, :512] = cv_pw[l].reshape(2, 128, 256).transpose(1, 0, 2).reshape(128, 512)
            elif u[0] == "wo_att":
                a = u[1]
                blk = np.zeros((128, 2, 4, 128), np.float32)
                for ml in range(2):
                    for kc in range(4):
                        blk[:, ml, kc, :] = w_out[l][pa[kc], (2 * a + ml) * 128:(2 * a + ml + 1) * 128]
                dst[:, :] = blk.reshape(128, 1024)
            elif u[0] in ("wo_cv", "wo_ret"):
                b = u[1]
                r0 = 512 if u[0] == "wo_cv" else 768
                blk = np.zeros((128, 4, 2, 128), np.float32)
                for ml in range(4):
                    for kc in range(2):
                        blk[:, ml, kc, :] = w_out[l][r0 + kc * 128:r0 + (kc + 1) * 128,
                                                     (4 * b + ml) * 128:(4 * b + ml + 1) * 128]
                dst[:, :] = blk.reshape(128, 1024)
            elif u[0] == "up":
                c, hu = u[1], u[2]
                dst[:, :] = kc_unit(ffn_up[l], hu * DFF + c * 128 + np.arange(128))
            elif u[0] == "down":
                cs, m = PARTS[u[1]], u[2]
                blk = np.zeros((128, 8, 128), np.float32)
                for ci, c in enumerate(cs):
                    blk[:, ci, :] = ffn_down[l][c * 128:(c + 1) * 128, m * 128:(m + 1) * 128]
                dst[:, :] = blk.reshape(128, 1024)
    return out


def pack_vecs(inp):
    pa = perm_att()
    v = np.zeros((128, DEPTH * NV), np.float32)
    p = np.arange(128)
    for l in range(DEPTH):
        o = l * NV
        v[:, o + VOFF["mix_g"]:o + VOFF["mix_g"] + 8] = inp["norm_mix_g"][l].reshape(8, 128).T
        v[:, o + VOFF["ffn_g"]:o + VOFF["ffn_g"] + 8] = inp["norm_ffn_g"][l].reshape(8, 128).T
        v[:, o + VOFF["qg"]] = inp["q_norm_g"][l][p % 64]
        v[:, o + VOFF["kg"]] = inp["k_norm_g"][l][p % 64]
        for c in range(4):
            v[:, o + VOFF["aog"] + c] = inp["attn_out_g"][l][pa[c]]
        for j in range(2):
            v[:, o + VOFF["cvw"] + j * 31:o + VOFF["cvw"] + (j + 1) * 31] = \
                inp["cv_dw_w"][l][:, j * 128:(j + 1) * 128].T
            v[:, o + VOFF["cvb"] + j] = inp["cv_dw_b"][l][j * 128:(j + 1) * 128]
            v[:, o + VOFF["lng"] + j] = inp["cv_ln_g"][l][j * 128:(j + 1) * 128]
            v[:, o + VOFF["lnb"] + j] = inp["cv_ln_b"][l][j * 128:(j + 1) * 128]
            v[:, o + VOFF["cog"] + j] = inp["cv_out_g"][l][j * 128:(j + 1) * 128]
            v[:, o + VOFF["gng"] + j] = inp["ret_gn_g"][l][j * 128:(j + 1) * 128]
        for c in range(NCF):
            for hu in range(2):
                u = 2 * c + hu
                col = hu * DFF + c * 128
                v[:, o + VOFF["fw"] + u * 3:o + VOFF["fw"] + u * 3 + 3] = \
                    inp["ffn_dw_w"][l][:, col:col + 128].T
                v[:, o + VOFF["fb"] + u] = inp["ffn_dw_b"][l][col:col + 128]
    return v


def const_tables():
    c = np.zeros((128, NCST), np.float64)
    j = np.arange(128)[:, None].astype(np.float64)
    i = np.arange(128)[None, :].astype(np.float64)
    attd = np.zeros((128, 5, 128))
    attd[:, 0, :] = np.where(i >= j, i - j, BIG)
    attd[:, 1, :] = np.where(j > i, 128 + i - j, BIG)
    attd[:, 2, :] = np.where(j >= 112, 128.0, BIG) + 0 * i
    attd[:, 3, :] = np.where(j >= 112, np.minimum(128 + i - j, 128.0), BIG)
    attd[:, 4, :] = np.where((j >= 112) & (i >= j), i - j, BIG)
    c[:, C_ATTD:C_ATTD + 640] = attd.reshape(128, 640)
    gam = np.array([1.0 - 2.0 ** (-5.0 - h) for h in range(4)])
    dec2 = np.zeros((128, 4, 128))
    for h in range(4):
        dec2[:, HPOS[h], :] = np.where(i >= j, gam[h] ** (-(j + 1.0)) / 8.0, 0.0)
    c[:, C_DEC2:C_DEC2 + 512] = dec2.reshape(128, 512)
    xi = np.zeros((128, 2, 128))
    cdec = np.zeros((128, 2))
    for p in range(128):
        for cc in range(2):
            h = 2 * cc + p // 64
            xi[p, cc, :] = gam[h] ** (np.arange(128) + 1.0)
            cdec[p, cc] = gam[h] ** 128.0
    c[:, C_XI:C_XI + 256] = xi.reshape(128, 256)
    zeta = np.zeros((128, 256))
    for h in range(4):
        zeta[:, h * 64:(h + 1) * 64] = (gam[h] ** (127.0 - np.arange(128)) / 8.0)[:, None]
    c[:, C_ZETA:C_ZETA + 256] = zeta
    c[:, C_CDEC:C_CDEC + 2] = cdec
    c[:, C_ID:C_ID + 128] = np.eye(128)
    return c.astype(np.float32)


def alibi_tables():
    a = np.zeros((128, NALB), np.float64)
    i = np.arange(128, dtype=np.float64)
    for g in range(2):
        pb = 64 * g
        for ty, off in ((0, 0.0), (1, 128.0)):
            for c in range(4):
                sl = SLOPES[4 * g + c]
                a[pb, ty * 512 + c * 128:ty * 512 + (c + 1) * 128] = -sl * (i + off)
                a[pb + 1, ty * 512 + c * 128:ty * 512 + (c + 1) * 128] = sl
        for c in range(4):
            a[pb, 2 * 512 + c * 128:2 * 512 + (c + 1) * 128] = -SLOPES[4 * g + c] * 128.0
        a[pb, A_A:A_A + 128] = 1.0
        a[pb + 1, A_A:A_A + 128] = i
    jj = np.arange(128)[:, None]
    ii = np.arange(128)[None, :]
    a[:, A_MC:A_MC + 128] = (ii >= jj)
    a[:, A_MP:A_MP + 128] = (jj > ii)
    a[112:128, A_OM:A_OM + 64] = 1.0
    return a.astype(np.float32)


class Ctx:
    pass


def tile_gi(t):
    return 0 if t == 0 else 1 + (t - 1) // 4


class _Stop(Exception):
    pass


def build_program(depth=DEPTH, dumps=(), stop=None):
    nc = bass.Bass("TRN2", target_bir_lowering=False)
    P = Prog(nc)
    cx = Ctx()
    cx.nc, cx.P = nc, P
    nunit_l = len(unit_schedule())
    NU = depth * nunit_l
    xT = nc.dram_tensor("xT", [D, S], F32, kind="ExternalInput").ap()
    metaT = nc.dram_tensor("metaT", [D, 16], F32, kind="ExternalInput").ap()
    wts = nc.dram_tensor("wts", [DEPTH * nunit_l, 128, 1024], F32, kind="ExternalInput").ap()
    vec_d = nc.dram_tensor("vec", [128, DEPTH * NV], F32, kind="ExternalInput").ap()
    cst_d = nc.dram_tensor("cst", [128, NCST], F32, kind="ExternalInput").ap()
    snk_d = nc.dram_tensor("snk", [DEPTH, 8], F32, kind="ExternalInput").ap()
    alb_d = nc.dram_tensor("alb", [128, NALB], F32, kind="ExternalInput").ap()
    outT = nc.dram_tensor("outT", [D, S], F32, kind="ExternalOutput").ap()
    hT = nc.alloc_sbuf_tensor("hT", [128, 8, T], F32)
    uT = nc.alloc_sbuf_tensor("uT", [128, 8, T], BF16)
    AR = nc.alloc_sbuf_tensor("AR", [128, 10, T], BF16)
    DG = nc.alloc_sbuf_tensor("DG", [128, 6, 128], BF16)
    W8 = nc.alloc_sbuf_tensor("W8", [128, NB, 1024], BF16)
    SF = nc.alloc_sbuf_tensor("SF", [128, NSF, 512], F32)
    SB = nc.alloc_sbuf_tensor("SB", [128, NSB, 516], BF16)
    CST = nc.alloc_sbuf_tensor("CST", [128, NCST], F32)
    VEC = nc.alloc_sbuf_tensor("VEC", [128, DEPTH * NV], F32)
    EPS = nc.alloc_sbuf_tensor("EPS", [128, 2], F32)
    GQ8 = nc.alloc_sbuf_tensor("GQ8", [128, DEPTH], F32)
    ESK = nc.alloc_sbuf_tensor("ESK", [128, DEPTH * 4], F32)
    ALB = nc.alloc_sbuf_tensor("ALB", [128, NALB], BF16)
    ONES = nc.alloc_sbuf_tensor("ONES", [128, 128], BF16)
    BONES = nc.alloc_sbuf_tensor("BONES", [128, 128], BF16)
    RF = nc.alloc_sbuf_tensor("RF", [128, 2, 64], F32)
    PS = nc.alloc_psum_tensor("PS", [128, 8, 512], F32)
    cx.psi = cx.sfi = cx.sbi = 0
    cx.u_issued = 0
    cx.u_cons = 0
    cx.dump_outs = []

    def ps_alloc():
        b = cx.psi % 8
        cx.psi += 1
        return b

    def run_pipe(gens, k):
        gens = list(gens)
        nxt, active = 0, []
        while nxt < len(gens) or active:
            keep = []
            for g in active:
                try:
                    next(g)
                    keep.append(g)
                except StopIteration:
                    pass
            active = keep
            if nxt < len(gens) and len(active) < k:
                g = gens[nxt]
                nxt += 1
                try:
                    next(g)
                    active.append(g)
                except StopIteration:
                    pass

    def ring(items):
        st = [0]

        def f():
            v = items[st[0] % len(items)]
            st[0] += 1
            return v
        return f

    def sf_alloc():
        b = cx.sfi % NSF
        cx.sfi += 1
        return b

    def sb_alloc():
        b = cx.sbi % NSB
        cx.sbi += 1
        return b

    def MM(out, lhsT, rhs, start, stop, reads, writes):
        P.add("pe", lambda e: e.matmul(out, lhsT=lhsT, rhs=rhs, start=start, stop=stop), reads, writes)

    def ACT(out, in_, func, reads, writes, scale=1.0, bias=None):
        if bias is None:
            P.add("act", lambda e: e.activation(out=out, in_=in_, func=func, scale=scale), reads, writes)
        else:
            P.add("act", lambda e: e.activation(out=out, in_=in_, func=func, scale=scale, bias=bias),
                  reads, writes)

    def TT(eng, out, in0, in1, op, reads, writes):
        P.add(eng, lambda e: e.tensor_tensor(out=out, in0=in0, in1=in1, op=op), reads, writes)

    def TS(eng, out, in0, s1, s2, op0, op1, reads, writes):
        if s2 is None:
            P.add(eng, lambda e: e.tensor_scalar(out=out, in0=in0, scalar1=s1, scalar2=None, op0=op0),
                  reads, writes)
        else:
            P.add(eng, lambda e: e.tensor_scalar(out=out, in0=in0, scalar1=s1, scalar2=s2, op0=op0, op1=op1),
                  reads, writes)

    def STT(out, in0, scalar, in1, op0, op1, reads, writes):
        P.add("dve", lambda e: e.scalar_tensor_tensor(out=out, in0=in0, scalar=scalar, in1=in1,
                                                      op0=op0, op1=op1), reads, writes)

    def RECIP(out, in_, reads, writes):
        P.add("dve", lambda e: e.reciprocal(out=out, in_=in_), reads, writes)

    def MEMSET(eng, ap, val, writes):
        P.add(eng, lambda e: e.memset(ap, val), (), writes)

    def DMA(eng, out, in_, reads, writes, dkey):
        return P.add(eng, lambda e: e.dma_start(out=out, in_=in_), reads, writes, dma=True, dkey=dkey)

    def stage(name):
        if stop == name:
            raise _Stop()

    def dump(name, ap, shape, dtype, reads):
        if name not in dumps:
            return
        dt = nc.dram_tensor("dbg_" + name, list(shape), dtype, kind="ExternalOutput").ap()
        o = P.add("sp", lambda e: e.dma_start(out=dt, in_=ap), reads, (), dma=True, dkey=("dbg", name))
        cx.dump_outs.append(o)

    def HK(c, gi):
        return ("h", c, gi)

    def UK(c, gi):
        return ("u", c, gi)

    def AK(slot, lo, n):
        return [("a", slot, t) for t in range(lo // 128, (lo + n + 127) // 128)]

    def AKF(slot0, flo, fn):
        ks = []
        a = flo
        while a < flo + fn:
            s = slot0 + a // T
            col = a % T
            ks.append(("a", s, col // 128))
            a = (a // 128 + 1) * 128
        return ks

    def A(slot, lo, n):
        return AR[:, slot, lo:lo + n]

    def vcol(l, name, i=0):
        o = l * NV + VOFF[name] + i
        return VEC[:, o:o + 1]

    def use_units(k):
        u0 = cx.u_cons
        lim = min(NU - 1, u0 - 1 + NB)
        while cx.u_issued <= lim:
            j = cx.u_issued
            s = j % NB
            P.add("pool", (lambda jj, ss: (lambda e: e.dma_start(out=W8[:, ss, :], in_=wts[jj, :, :])))(j, s),
                  (), [("w", s)], dma=True, dkey=("w", s))
            cx.u_issued += 1
        cx.u_cons += k
        assert cx.u_issued >= cx.u_cons
        return [(u0 + i) % NB for i in range(k)]

    o_c = P.add("sp", lambda e: [e.dma_start(out=CST[:, :], in_=cst_d[:, :]),
                                 e.dma_start(out=VEC[:, :], in_=vec_d[:, :])],
                (), ["cst", "vec"], dma=True, dkey="cst", ndma=2)

    def snk_load(e):
        r = []
        for l in range(DEPTH):
            for g in range(2):
                r.append(e.dma_start(out=ESK[64 * g:64 * g + 64, 4 * l:4 * l + 4],
                                     in_=snk_d[l:l + 1, 4 * g:4 * g + 4].partition_broadcast(64)))
        return r
    P.add("sp", snk_load, (), ["esk"], dma=True, dkey="snk", ndma=2 * DEPTH)
    P.add("pool", lambda e: e.dma_start(out=ALB[:, :], in_=alb_d[:, :]), (), ["alb"], dma=True, dkey="alb")
    for c in range(8):
        P.add("sp", (lambda cc: (lambda e: [e.dma_start(out=hT[:, cc, 128:T], in_=xT[cc * 128:(cc + 1) * 128, :]),
                                            e.dma_start(out=hT[:, cc, 112:128], in_=metaT[cc * 128:(cc + 1) * 128, :])]))(c),
              (), [HK(c, gi) for gi in range(5)], dma=True, dkey=("hin", c), ndma=2)
    MEMSET("pool", EPS[:, 0:1], RMS_EPS, ["eps"])
    MEMSET("pool", EPS[:, 1:2], LN_EPS, ["eps"])
    MEMSET("pool", ONES[:, :], 1.0, ["ones"])
    MEMSET("pool", BONES[:, :], 0.0, ["bones"])
    MEMSET("pool", BONES[0:64, 0:64], 1.0, ["bones"])
    MEMSET("pool", BONES[64:128, 64:128], 1.0, ["bones"])
    for c in range(8):
        MEMSET("dve", hT[:, c, 0:112], 0.0, [("hpad", c)])
        MEMSET("pool", uT[:, c, 0:112], 0.0, [UK(c, 0)])
    for sl in range(10):
        MEMSET("pool", AR[:, sl, :], 0.0, AK(sl, 0, T))
    for i in range(NSF):
        MEMSET("pool", SF[:, i, :], 0.0, [("sf", i)])
    for i in range(NSB):
        MEMSET("pool", SB[:, i, :], 0.0, [("sb", i), ("sbh", i)])
    ACT(ESK[:, :], ESK[:, :], AF.Exp, ["esk"], ["esk"])
    for l in range(depth):
        TS("dve", GQ8[:, l:l + 1], vcol(l, "qg"), 0.125, None, ALU.mult, None, ["vec"], ["gq8"])

    def rstd_from_ps(b, n, inv_n, eps_i, extra_reads=()):
        i = sf_alloc()
        ACT(SF[:, i, 0:n], PS[:, b, 0:n], AF.Ln, [("ps", b), "eps"] + list(extra_reads), [("sf", i)],
            scale=inv_n, bias=EPS[:, eps_i:eps_i + 1])
        ACT(SF[:, i, 0:n], SF[:, i, 0:n], AF.Exp, [("sf", i)], [("sf", i)], scale=-0.5)
        return i

    def rmsnorm_main(l, gname):
        def it(gi, lo, n):
            b = ps_alloc()
            for c in range(8):
                s = sb_alloc()
                ACT(SB[:, s, 0:n], hT[:, c, lo:lo + n], AF.Square, [HK(c, gi), ("hpad", c)], [("sb", s)])
                MM(PS[:, b, 0:n], ONES[:, :], SB[:, s, 0:n], c == 0, c == 7, [("sb", s), "ones"], [("ps", b)])
            yield
            r = rstd_from_ps(b, n, 1.0 / D, 0)
            yield
            lo2, n2 = (112, 16) if gi == 0 else (lo, n)
            off = lo2 - lo
            for c in range(8):
                STT(uT[:, c, lo2:lo2 + n2], hT[:, c, lo2:lo2 + n2], vcol(l, gname, c), SF[:, r, off:off + n2],
                    ALU.mult, ALU.mult, [HK(c, gi), ("sf", r), "vec"], [UK(c, gi)])
        run_pipe([it(gi, lo, n) for gi, (lo, n) in enumerate(GRP)], 2)

    def fm_proj(s, gi, b, grp=GRP):
        lo, n = grp[gi]
        for kc in range(8):
            MM(PS[:, b, 0:n], W8[:, s, kc * 128:(kc + 1) * 128], uT[:, kc, lo:lo + n], kc == 0, kc == 7,
               [("w", s), UK(kc, gi)], [("ps", b)])

    def norm_full(l, slots, gname, nfeat):
        def it(gi, lo, n):
            b = ps_alloc()
            for ci, sl in enumerate(slots):
                s = sb_alloc()
                ACT(SB[:, s, 0:n], A(sl, lo, n), AF.Square, AK(sl, lo, n), [("sb", s)])
                MM(PS[:, b, 0:n], ONES[:, :], SB[:, s, 0:n], ci == 0, ci == len(slots) - 1,
                   [("sb", s), "ones"], [("ps", b)])
            yield
            r = rstd_from_ps(b, n, 1.0 / nfeat, 0)
            yield
            for ci, sl in enumerate(slots):
                STT(A(sl, lo, n), A(sl, lo, n), vcol(l, gname, ci), SF[:, r, 0:n], ALU.mult, ALU.mult,
                    AK(sl, lo, n) + [("sf", r), "vec"], AK(sl, lo, n))
        run_pipe([it(gi, lo, n) for gi, (lo, n) in enumerate(GRP)], 2)

    def wout_partial(l, kind, yslots):
        nk = len(yslots)
        mper = 4 if nk == 2 else 2
        nun = 8 // mper
        for ub in range(nun):
            (s,) = use_units(1)
            for ml in range(mper):
                m = ub * mper + ml
                for gi, (lo, n) in enumerate(GRP16):
                    b = ps_alloc()
                    for kc in range(nk):
                        col = (ml * nk + kc) * 128
                        MM(PS[:, b, 0:n], W8[:, s, col:col + 128], A(yslots[kc], lo, n), kc == 0, kc == nk - 1,
                           [("w", s)] + AK(yslots[kc], lo, n), [("ps", b)])
                    TT("dve", hT[:, m, lo:lo + n], hT[:, m, lo:lo + n], PS[:, b, 0:n], ALU.add,
                       [HK(m, gi), ("ps", b)], [HK(m, gi)])

    CD = lambda off, n: CST[:, off:off + n]
    out_ops = []
    try:
      stage("pro")
      for l in range(depth):
          last = (l == depth - 1)
          rmsnorm_main(l, "mix_g")
          dump("u%d" % l, uT[:, :, :], [128, 8, T], BF16, [UK(c, gi) for c in range(8) for gi in range(5)])
          stage("norm1")
          su = use_units(5)
          stage("tm0")
          AR01 = AR[:, 0:2, :].rearrange("p a b -> p (a b)")
          AR23 = AR[:, 2:4, :].rearrange("p a b -> p (a b)")
          for t in range(NT):
              stage("tmt%d" % t)
              gi = tile_gi(t)
              b0 = ps_alloc()
              b1 = ps_alloc()
              for ui in range(5):
                  s = su[ui]
                  dst = PS[:, b0, ui * 128:(ui + 1) * 128] if ui < 4 else PS[:, b1, 0:128]
                  bb = b0 if ui < 4 else b1
                  for kc in range(8):
                      MM(dst, uT[:, kc, t * 128:(t + 1) * 128], W8[:, s, kc * 128:(kc + 1) * 128], kc == 0, kc == 7,
                         [("w", s), UK(kc, gi)], [("ps", bb)])
              ACT(A(9, t * 128, 128), PS[:, b0, 0:128], AF.Copy, [("ps", b0)], AK(9, t * 128, 128))
              ACT(AR01[:, t * 256:(t + 1) * 256], PS[:, b0, 128:384], AF.Copy, [("ps", b0)], AKF(0, t * 256, 256))
              TT("dve", AR23[:, t * 256:t * 256 + 128], PS[:, b0, 384:512], CD(C_ZETA, 128), ALU.mult,
                 [("ps", b0), "cst"], AKF(2, t * 256, 128))
              TT("dve", AR23[:, t * 256 + 128:t * 256 + 256], PS[:, b1, 0:128], CD(C_ZETA + 128, 128), ALU.mult,
                 [("ps", b1), "cst"], AKF(2, t * 256 + 128, 128))
          dump("vtm%d" % l, AR[:, 9, :], [128, T], BF16, AK(9, 0, T))
          dump("rvtm%d" % l, AR01, [128, 2 * T], BF16, AK(0, 0, T) + AK(1, 0, T))
          dump("kztm%d" % l, AR23, [128, 2 * T], BF16, AK(2, 0, T) + AK(3, 0, T))
          stage("tm")

          for j in range(2):
              sa, sbw = use_units(2)
              for gi, (lo, n) in enumerate(GRP):
                  ba, bb = ps_alloc(), ps_alloc()
                  fm_proj(sa, gi, ba)
                  fm_proj(sbw, gi, bb)
                  i = sf_alloc()
                  ACT(SF[:, i, 0:n], PS[:, bb, 0:n], AF.Sigmoid, [("ps", bb)], [("sf", i)])
                  TT("dve", A(4 + j, lo, n), PS[:, ba, 0:n], SF[:, i, 0:n], ALU.mult,
                     [("ps", ba), ("sf", i)], AK(4 + j, lo, n))
          stage("conv1")
          ps_scan = ring([5, 6, 7])

          def scan_gen():
              MEMSET("dve", RF[:, :, :], 0.0, ["rf"])
              for t in range(NT - 1):
                  b = ps_scan()
                  for h in range(4):
                      pb, cc = 64 * (h % 2), h // 2
                      MM(PS[pb:pb + 64, b, cc * 64:(cc + 1) * 64],
                         AR23[:, t * 256 + h * 64:t * 256 + (h + 1) * 64],
                         AR01[:, t * 256 + h * 64:t * 256 + (h + 1) * 64], True, True,
                         AKF(2, t * 256 + h * 64, 64) + AKF(0, t * 256 + h * 64, 64), [("ps", b)])
                  for cc in range(2):
                      STT(RF[:, cc, :], RF[:, cc, :], CST[:, C_CDEC + cc:C_CDEC + cc + 1], PS[:, b, cc * 64:(cc + 1) * 64],
                          ALU.mult, ALU.add, ["rf", ("ps", b), "cst"], ["rf"])
                  ACT(A(8, (t + 1) * 128, 128), RF[:, :, :].rearrange("p a b -> p (a b)"), AF.Copy, ["rf"],
                      AK(8, (t + 1) * 128, 128))
                  yield
          scan = scan_gen()
          CG = [(112, 16)] + GRP[1:]
          dg_ring = ring(list(range(6)))
          for j in range(2):
              wc = l * NV + VOFF["cvw"] + j * 31
              banks = [0, 1, 2, 3, 4]
              for k in range(31):
                  sh = 30 - k
                  if k % 4 == 1:
                      next(scan, None)
                  d = dg_ring()
                  eng_d = "dve" if k % 2 == 0 else "act"
                  if eng_d == "dve":
                      TS("dve", DG[:, d, :], CST[:, C_ID:C_ID + 128], VEC[:, wc + k:wc + k + 1], None, ALU.mult, None,
                         ["cst", "vec"], [("dg", d)])
                  else:
                      ACT(DG[:, d, :], CST[:, C_ID:C_ID + 128], AF.Copy, ["cst", "vec"], [("dg", d)],
                          scale=VEC[:, wc + k:wc + k + 1])
                  for gi, (lo, n) in enumerate(CG):
                      MM(PS[:, banks[gi], 0:n], DG[:, d, :], A(4 + j, lo - sh, n), k == 0, k == 30,
                         [("dg", d)] + AK(4 + j, lo - sh, n), [("ps", banks[gi])])
              for gi, (lo, n) in enumerate(CG):
                  ACT(A(6 + j, lo, n), PS[:, banks[gi], 0:n], AF.Identity, [("ps", banks[gi]), "vec"], AK(6 + j, lo, n),
                      bias=vcol(l, "cvb", j))
          dump("cvo%d" % l, AR[:, 6:8, :], [128, 2, T], BF16, AK(6, 0, T) + AK(7, 0, T))
          stage("conv2")
          for _ in scan:
              pass
          def cln_it(gi, lo, n):
              bm, bs = ps_alloc(), ps_alloc()
              for j in range(2):
                  s = sb_alloc()
                  ACT(SB[:, s, 0:n], A(6 + j, lo, n), AF.Square, AK(6 + j, lo, n), [("sb", s)])
                  MM(PS[:, bm, 0:n], ONES[:, :], A(6 + j, lo, n), j == 0, j == 1, AK(6 + j, lo, n) + ["ones"], [("ps", bm)])
                  MM(PS[:, bs, 0:n], ONES[:, :], SB[:, s, 0:n], j == 0, j == 1, [("sb", s), "ones"], [("ps", bs)])
              i1 = sf_alloc()
              i2 = sf_alloc()
              yield
              ACT(SF[:, i1, 0:n], PS[:, bm, 0:n], AF.Identity, [("ps", bm)], [("sf", i1)], scale=-1.0 / 256)
              yield
              TT("dve", SF[:, i2, 0:n], SF[:, i1, 0:n], SF[:, i1, 0:n], ALU.mult, [("sf", i1)], [("sf", i2)])
              STT(SF[:, i2, 0:n], PS[:, bs, 0:n], 1.0 / 256, SF[:, i2, 0:n], ALU.mult, ALU.subtract,
                  [("ps", bs), ("sf", i2)], [("sf", i2)])
              i3s = [sf_alloc(), sf_alloc()]
              for j in range(2):
                  TT("dve", SF[:, i3s[j], 0:n], A(6 + j, lo, n), SF[:, i1, 0:n], ALU.add,
                     AK(6 + j, lo, n) + [("sf", i1)], [("sf", i3s[j])])
              yield
              ACT(SF[:, i2, 0:n], SF[:, i2, 0:n], AF.Ln, [("sf", i2), "eps"], [("sf", i2)], bias=EPS[:, 1:2])
              ACT(SF[:, i2, 0:n], SF[:, i2, 0:n], AF.Exp, [("sf", i2)], [("sf", i2)], scale=-0.5)
              yield
              for j in range(2):
                  TT("dve", SF[:, i3s[j], 0:n], SF[:, i3s[j], 0:n], SF[:, i2, 0:n], ALU.mult,
                     [("sf", i3s[j]), ("sf", i2)], [("sf", i3s[j])])
              yield
              for j in range(2):
                  ACT(A(4 + j, lo, n), SF[:, i3s[j], 0:n], AF.Silu, [("sf", i3s[j]), "vec"], AK(4 + j, lo, n),
                      scale=vcol(l, "lng", j), bias=vcol(l, "lnb", j))
          run_pipe([cln_it(gi, lo, n) for gi, (lo, n) in enumerate(CG)], 2)
          (spw,) = use_units(1)
          for m in range(2):
              for gi, (lo, n) in enumerate(CG):
                  b = ps_alloc()
                  for kc in range(2):
                      col = (kc * 2 + m) * 128
                      MM(PS[:, b, 0:n], W8[:, spw, col:col + 128], A(4 + kc, lo, n), kc == 0, kc == 1,
                         [("w", spw)] + AK(4 + kc, lo, n), [("ps", b)])
                  ACT(A(6 + m, lo, n), PS[:, b, 0:n], AF.Copy, [("ps", b)], AK(6 + m, lo, n))
          norm_full(l, [6, 7], "cog", 256)
          dump("ycv%d" % l, AR[:, 6:8, :], [128, 2, T], BF16, AK(6, 0, T) + AK(7, 0, T))
          stage("cv")
          wout_partial(l, "cv", [6, 7])

          su = use_units(4)
          for c in range(2):
              for gi, (lo, n) in enumerate(GRP):
                  b = ps_alloc()
                  fm_proj(su[c], gi, b)
                  ntl = n // 128
                  TT("dve", A(4 + c, lo, n).rearrange("p (a b) -> p a b", b=128),
                     PS[:, b, 0:n].rearrange("p (a b) -> p a b", b=128),
                     CST[:, C_XI + c * 128:C_XI + (c + 1) * 128].unsqueeze(1).to_broadcast([128, ntl, 128]),
                     ALU.mult, [("ps", b), "cst"], AK(4 + c, lo, n))
          for c in range(2):
              for gi, (lo, n) in enumerate(GRP):
                  b = ps_alloc()
                  fm_proj(su[2 + c], gi, b)
                  ACT(A(6 + c, lo, n), PS[:, b, 0:n], AF.Copy, [("ps", b)], AK(6 + c, lo, n))
          stage("ret1")
          stage("ret2")
          ps_rs = ring([0, 1, 2, 3])
          ps_ry = ring([4, 5, 6, 7])
          def ret_it(t):
              bSe, bSo = ps_rs(), ps_rs()
              for h in range(4):
                  pb, cc = 64 * (h % 2), h // 2
                  bS = bSe if h % 2 == 0 else bSo
                  MM(PS[:, bS, cc * 128:(cc + 1) * 128], AR[pb:pb + 64, 6 + cc, t * 128:(t + 1) * 128],
                     AR[pb:pb + 64, 4 + cc, t * 128:(t + 1) * 128], True, True,
                     AK(6 + cc, t * 128, 128) + AK(4 + cc, t * 128, 128), [("ps", bS)])
              yield
              s = sb_alloc()
              TT("dve", SB[:, s, 0:256], PS[:, bSe, 0:256], CD(C_DEC2, 256), ALU.mult, [("ps", bSe), "cst"], [("sb", s)])
              TT("dve", SB[:, s, 256:512], PS[:, bSo, 0:256], CD(C_DEC2 + 256, 256), ALU.mult,
                 [("ps", bSo), "cst"], [("sb", s)])
              yield
              bY = ps_ry()
              for h in range(4):
                  pb, cc = 64 * (h % 2), h // 2
                  MM(PS[pb:pb + 64, bY, cc * 128:(cc + 1) * 128], AR01[:, t * 256 + h * 64:t * 256 + (h + 1) * 64],
                     SB[:, s, HPOS[h] * 128:(HPOS[h] + 1) * 128], True, t == 0,
                     AKF(0, t * 256 + h * 64, 64) + [("sb", s)], [("ps", bY)])
                  if t > 0:
                      MM(PS[pb:pb + 64, bY, cc * 128:(cc + 1) * 128],
                         AR[pb:pb + 64, 8, t * 128 + cc * 64:t * 128 + (cc + 1) * 64],
                         AR[pb:pb + 64, 4 + cc, t * 128:(t + 1) * 128], False, True,
                         AK(8, t * 128, 128) + AK(4 + cc, t * 128, 128), [("ps", bY)])
              yield
              ACT(AR[:, 2:4, t * 128:(t + 1) * 128], PS[:, bY, 0:256].rearrange("p (a b) -> p a b", b=128), AF.Copy,
                  [("ps", bY)], AK(2, t * 128, 128) + AK(3, t * 128, 128))
          run_pipe([ret_it(t) for t in range(NT)], 2)
          dump("yretraw%d" % l, AR[:, 2:4, :], [128, 2, T], BF16, AK(2, 0, T) + AK(3, 0, T))
          stage("ret3")
          srg = use_units(2)
          def gn_it(c, gi, lo, n):
              bm, bs, bg = ps_alloc(), ps_alloc(), ps_alloc()
              s = sb_alloc()
              ACT(SB[:, s, 0:n], A(2 + c, lo, n), AF.Square, AK(2 + c, lo, n), [("sb", s)])
              MM(PS[:, bm, 0:n], BONES[:, :], A(2 + c, lo, n), True, True, AK(2 + c, lo, n) + ["bones"], [("ps", bm)])
              MM(PS[:, bs, 0:n], BONES[:, :], SB[:, s, 0:n], True, True, [("sb", s), "bones"], [("ps", bs)])
              fm_proj(srg[c], gi, bg)
              i1, i2, i3, i4 = sf_alloc(), sf_alloc(), sf_alloc(), sf_alloc()
              yield
              ACT(SF[:, i1, 0:n], PS[:, bm, 0:n], AF.Identity, [("ps", bm)], [("sf", i1)], scale=-1.0 / 64)
              ACT(SF[:, i4, 0:n], PS[:, bg, 0:n], AF.Silu, [("ps", bg)], [("sf", i4)])
              yield
              TT("dve", SF[:, i2, 0:n], SF[:, i1, 0:n], SF[:, i1, 0:n], ALU.mult, [("sf", i1)], [("sf", i2)])
              STT(SF[:, i2, 0:n], PS[:, bs, 0:n], 1.0 / 64, SF[:, i2, 0:n], ALU.mult, ALU.subtract,
                  [("ps", bs), ("sf", i2)], [("sf", i2)])
              TT("dve", SF[:, i3, 0:n], A(2 + c, lo, n), SF[:, i1, 0:n], ALU.add,
                 AK(2 + c, lo, n) + [("sf", i1)], [("sf", i3)])
              yield
              ACT(SF[:, i2, 0:n], SF[:, i2, 0:n], AF.Ln, [("sf", i2), "eps"], [("sf", i2)], bias=EPS[:, 1:2])
              ACT(SF[:, i2, 0:n], SF[:, i2, 0:n], AF.Exp, [("sf", i2)], [("sf", i2)], scale=-0.5)
              yield
              STT(SF[:, i3, 0:n], SF[:, i3, 0:n], vcol(l, "gng", c), SF[:, i2, 0:n], ALU.mult, ALU.mult,
                  [("sf", i3), ("sf", i2), "vec"], [("sf", i3)])
              TT("dve", A(2 + c, lo, n), SF[:, i3, 0:n], SF[:, i4, 0:n], ALU.mult, [("sf", i3), ("sf", i4)],
                 AK(2 + c, lo, n))
          run_pipe([gn_it(c, gi, lo, n) for c in range(2) for gi, (lo, n) in enumerate(GRP)], 2)
          dump("yret%d" % l, AR[:, 2:4, :], [128, 2, T], BF16, AK(2, 0, T) + AK(3, 0, T))
          stage("ret")
          wout_partial(l, "ret", [2, 3])

          su = use_units(5)
          def qk_it(c, gi, lo, n):
              gcol = GQ8[:, l:l + 1] if c < 4 else vcol(l, "kg")
              b = ps_alloc()
              fm_proj(su[c], gi, b)
              s = sb_alloc()
              ACT(SB[:, s, 0:n], PS[:, b, 0:n], AF.Square, [("ps", b)], [("sb", s)])
              yield
              b2 = ps_alloc()
              MM(PS[:, b2, 0:n], BONES[:, :], SB[:, s, 0:n], True, True, [("sb", s), "bones"], [("ps", b2)])
              r = rstd_from_ps(b2, n, 1.0 / 64, 0)
              yield
              STT(A(c, lo, n), PS[:, b, 0:n], gcol, SF[:, r, 0:n], ALU.mult, ALU.mult,
                  [("ps", b), ("sf", r), "vec", "gq8"], AK(c, lo, n))
          run_pipe([qk_it(c, gi, lo, n) for c in range(5) for gi, (lo, n) in enumerate(GRP)], 3)
          dump("qk%d" % l, AR[:, 0:5, :], [128, 5, T], BF16, [k for c in range(5) for k in AK(c, 0, T)])
          ps_s = ring([0, 1, 2, 3])
          ps_nd = ring([4, 5, 6, 7])
          def att_it(t):
              if t == 0:
                  chunks = [("old", 4, 0)]
              elif t == 1:
                  chunks = [("old", 3, 0), ("new", 0, 1)]
              else:
                  chunks = [("new", 2, 0), ("new", 1, t - 1), ("new", 0, t)]
              bN, bD = ps_nd(), ps_nd()
              pts = {0: [], 1: []}
              for g in range(2):
                  pb = 64 * g
                  for (mode, ty, kt) in chunks:
                      bS = ps_s()
                      MM(PS[:, bS, 0:512].rearrange("p (a b) -> p a b", b=128),
                         AR[pb:pb + 64, 4, kt * 128:(kt + 1) * 128],
                         AR[pb:pb + 64, 0:4, t * 128:(t + 1) * 128], True, mode == "old",
                         AK(4, kt * 128, 128) + [k for c in range(4) for k in AK(c, t * 128, 128)], [("ps", bS)])
                      s = sb_alloc()
                      if mode == "old":
                          i = sf_alloc()
                          for c in range(4):
                              STT(SF[:, i, c * 128:(c + 1) * 128], CST[:, C_ATTD + ty * 128:C_ATTD + (ty + 1) * 128],
                                  -SLOPES[4 * g + c], PS[:, bS, c * 128:(c + 1) * 128], ALU.mult, ALU.add,
                                  [("ps", bS), "cst"], [("sf", i)])
                          ACT(SB[:, s, 0:512], SF[:, i, 0:512], AF.Exp, [("sf", i)], [("sb", s)])
                      else:
                          MM(PS[:, bS, 0:512], ALB[pb:pb + 2, A_A:A_A + 128], ALB[pb:pb + 2, ty * 512:(ty + 1) * 512],
                             False, True, ["alb"], [("ps", bS)])
                          ACT(SB[:, s, 0:512], PS[:, bS, 0:512], AF.Exp, [("ps", bS)], [("sb", s)])
                          if ty in (0, 1):
                              mcol = A_MC if ty == 0 else A_MP
                              TT("dve" if ty == 0 else "pool", SB[:, s, 0:512].rearrange("p (a b) -> p a b", b=128),
                                 SB[:, s, 0:512].rearrange("p (a b) -> p a b", b=128),
                                 ALB[:, mcol:mcol + 128].unsqueeze(1).to_broadcast([128, 4, 128]), ALU.mult,
                                 [("sb", s), "alb"], [("sb", s)])
                      pts[g].append((s, kt, mode, ty))
              yield
              for g in range(2):
                  pb = 64 * g
                  for ci, (s, kt, mode, ty) in enumerate(pts[g]):
                      MM(PS[pb:pb + 64, bN, 0:512], AR[:, 9, kt * 128 + pb:kt * 128 + pb + 64], SB[:, s, 0:512],
                         ci == 0, ci == len(pts[g]) - 1, AK(9, kt * 128, 128) + [("sb", s)], [("ps", bN)])
                  for ci, (s, kt, mode, ty) in enumerate(pts[g]):
                      ones_ap = ALB[:, A_OM:A_OM + 64] if (mode == "new" and ty == 2) else ONES[:, 0:64]
                      MM(PS[pb:pb + 64, bD, 0:512], ones_ap, SB[:, s, 0:512],
                         ci == 0, ci == len(pts[g]) - 1, ["ones", "alb", ("sb", s)], [("ps", bD)])
              yield
              i = sf_alloc()
              TT("dve", SF[:, i, 0:512].rearrange("p (a b) -> p a b", b=128),
                 PS[:, bD, 0:512].rearrange("p (a b) -> p a b", b=128),
                 ESK[:, 4 * l:4 * l + 4].unsqueeze(2).to_broadcast([128, 4, 128]), ALU.add,
                 [("ps", bD), "esk"], [("sf", i)])
              ACT(SF[:, i, 0:512], SF[:, i, 0:512], AF.Ln, [("sf", i)], [("sf", i)])
              yield
              ACT(SF[:, i, 0:512], SF[:, i, 0:512], AF.Exp, [("sf", i)], [("sf", i)], scale=-1.0)
              TT("dve", AR[:, 5:9, t * 128:(t + 1) * 128], PS[:, bN, 0:512].rearrange("p (a b) -> p a b", b=128),
                 SF[:, i, 0:512].rearrange("p (a b) -> p a b", b=128), ALU.mult,
                 [("ps", bN), ("sf", i)], [k for c in range(5, 9) for k in AK(c, t * 128, 128)])
          run_pipe([att_it(t) for t in range(NT)], 2)
          dump("yattraw%d" % l, AR[:, 5:9, :], [128, 4, T], BF16, [k for c in range(5, 9) for k in AK(c, 0, T)])
          norm_full(l, [5, 6, 7, 8], "aog", 512)
          wout_partial(l, "att", [5, 6, 7, 8])
          dump("hmid%d" % l, hT[:, :, :], [128, 8, T], F32, [HK(c, gi) for c in range(8) for gi in range(5)])
          stage("att")

          rmsnorm_main(l, "ffn_g")
          ps_up = ring([0, 1, 2, 3, 4, 5])
          ps_dn = ring([6, 7])
          for pi, part in enumerate(PARTS):
              ffn_st = {}

              def ffn_it(ci, c, gi, lo, n):
                  if gi == 0:
                      ffn_st[c] = {"w": use_units(2), "nxt": None}
                  st = ffn_st[c]
                  sg, su_ = st["w"]
                  bg, bu = ps_up(), ps_up()
                  fm_proj(sg, gi, bg, GRP16)
                  fm_proj(su_, gi, bu, GRP16)
                  if gi == 0:
                      cur = [sb_alloc(), sb_alloc()]
                      for hu in range(2):
                          ACT(SB[:, cur[hu], 0:2], EPS[:, 0:2], AF.Copy, ["eps"], [("sbh", cur[hu])], scale=0.0)
                  else:
                      cur = st["nxt"]
                  nxt = [sb_alloc(), sb_alloc()] if gi < len(GRP16) - 1 else None
                  st["nxt"] = nxt
                  accs = []
                  for hu, b in ((0, bg), (1, bu)):
                      u = 2 * c + hu
                      wo = l * NV + VOFF["fw"] + u * 3
                      sbf = cur[hu]
                      ACT(SB[:, sbf, 2:2 + n], PS[:, b, 0:n], AF.Copy, [("ps", b)], [("sb", sbf)])
                      if nxt is not None:
                          ACT(SB[:, nxt[hu], 0:2], PS[:, b, n - 2:n], AF.Copy, [("ps", b)], [("sbh", nxt[hu])])
                      i = sf_alloc()
                      ACT(SF[:, i, 0:n], PS[:, b, 0:n], AF.Identity, [("ps", b), "vec"], [("sf", i)],
                          scale=VEC[:, wo + 2:wo + 3], bias=vcol(l, "fb", u))
                      accs.append(i)
                  yield
                  for hu in range(2):
                      u = 2 * c + hu
                      wo = l * NV + VOFF["fw"] + u * 3
                      sbf, i = cur[hu], accs[hu]
                      STT(SF[:, i, 0:n], SB[:, sbf, 1:1 + n], VEC[:, wo + 1:wo + 2], SF[:, i, 0:n], ALU.mult, ALU.add,
                          [("sb", sbf), ("sbh", sbf), ("sf", i), "vec"], [("sf", i)])
                      STT(SF[:, i, 0:n], SB[:, sbf, 0:n], VEC[:, wo:wo + 1], SF[:, i, 0:n], ALU.mult, ALU.add,
                          [("sb", sbf), ("sbh", sbf), ("sf", i), "vec"], [("sf", i)])
                  yield
                  ig, iu = accs
                  ACT(SF[:, ig, 0:n], SF[:, ig, 0:n], AF.Silu, [("sf", ig)], [("sf", ig)])
                  yield
                  TT("pool", A(ci, lo, n), SF[:, ig, 0:n], SF[:, iu, 0:n], ALU.mult, [("sf", ig), ("sf", iu)],
                     AK(ci, lo, n))
              run_pipe([ffn_it(ci, c, gi, lo, n) for ci, c in enumerate(part) for gi, (lo, n) in enumerate(GRP16)], 3)
              if pi == 0:
                  dump("g%d" % l, AR[:, 0:8, :], [128, 8, T], BF16, [k for c in range(8) for k in AK(c, 0, T)])
              npc = len(part)
              for m in range(8):
                  (s,) = use_units(1)
                  for gi, (lo, n) in enumerate(GRP16):
                      b = ps_dn()
                      for ci in range(npc):
                          MM(PS[:, b, 0:n], W8[:, s, ci * 128:(ci + 1) * 128], A(ci, lo, n), ci == 0, ci == npc - 1,
                             [("w", s)] + AK(ci, lo, n), [("ps", b)])
                      TT("dve", hT[:, m, lo:lo + n], hT[:, m, lo:lo + n], PS[:, b, 0:n], ALU.add,
                         [HK(m, gi), ("ps", b)], [HK(m, gi)])
                  if last and pi == len(PARTS) - 1:
                      out_ops.append(DMA("sp", outT[m * 128:(m + 1) * 128, :], hT[:, m, 128:T],
                                         [HK(m, gi) for gi in range(1, 5)], (), ("out", m)))
          dump("hout%d" % l, hT[:, :, :], [128, 8, T], F32, [HK(c, gi) for c in range(8) for gi in range(5)])
    except _Stop:
        out_ops = [DMA("sp", outT[m * 128:(m + 1) * 128, :], hT[:, m, 128:T],
                       [HK(m, gi) for gi in range(1, 5)], (), ("out", m)) for m in range(8)]
    P.emit(final_wait_ops=out_ops + cx.dump_outs)
    return nc


_CACHE = {}


def make_in_maps(inp):
    x = np.asarray(inp["x"], np.float32)
    wts = pack_units(np.asarray(inp["w_in"], np.float32), np.asarray(inp["cv_pw"], np.float32),
                     np.asarray(inp["w_out"], np.float32), np.asarray(inp["ffn_up"], np.float32),
                     np.asarray(inp["ffn_down"], np.float32))
    vec = pack_vecs({k: np.asarray(v, np.float32) for k, v in inp.items()})
    cst = const_tables()
    alb = alibi_tables()
    metaT = np.ascontiguousarray(np.asarray(inp["meta"], np.float32).T)
    snk = np.ascontiguousarray(np.asarray(inp["attn_sinks"], np.float32))
    maps = []
    for b in range(8):
        maps.append({"xT": np.ascontiguousarray(x[b].T), "metaT": metaT, "wts": wts, "vec": vec,
                     "cst": cst, "snk": snk, "alb": alb})
    return maps


def kernel(**inputs):
    if "nc" not in _CACHE:
        _CACHE["nc"] = build_program()
    nc = _CACHE["nc"]
    maps = make_in_maps(inputs)
    res = run_bass_kernel_spmd(nc, maps, core_ids=list(range(8)))
    out = np.stack([np.ascontiguousarray(r["outT"].T) for r in res.results], axis=0)
    return out.astype(np.float32)
```

```python
import numpy as np
import concourse.bass as bass
import concourse.mybir as mybir
from concourse.bass_utils import run_bass_kernel_spmd

F32 = mybir.dt.float32
BF16 = mybir.dt.bfloat16
AF = mybir.ActivationFunctionType
ALU = mybir.AluOpType

D = 1024
S = 2048
T = 2176
NT = 17
DEPTH = 2
DFF = 2816
NCF = 22
IN_W = 2304
GRP = [(0, 128), (128, 512), (640, 512), (1152, 512), (1664, 512)]
GRP16 = [(112, 16)] + GRP[1:]
PARTS = [list(range(0, 8)), list(range(8, 15)), list(range(15, 22))]
NB = 6
NSF = 11
NSB = 12
BIG = 1.0e9
SLOPES = [float(2.0 ** (-8.0 * (i + 1) / 8.0)) for i in range(8)]
RMS_EPS = 1e-6
HPOS = {0: 0, 2: 1, 1: 2, 3: 3}
LN_EPS = 1e-5


class Op:
    __slots__ = ("eng", "fn", "reads", "writes", "dma", "dkey", "ndma", "deps",
                 "signal", "sem", "val", "vc", "idx")


class Prog:
    def __init__(self, nc):
        self.nc = nc
        self.ops = []
        self.last_w = {}
        self.readers = {}
        self.last_acc = {}
        self.engs = {"pe": nc.tensor, "dve": nc.vector, "act": nc.scalar,
                     "pool": nc.gpsimd, "sp": nc.sync}

    def add(self, eng, fn, reads=(), writes=(), dma=False, dkey=None, ndma=1):
        o = Op()
        o.eng, o.fn, o.dma, o.dkey, o.ndma = eng, fn, dma, dkey, ndma
        o.reads, o.writes = tuple(reads), tuple(writes)
        o.idx = len(self.ops)
        o.signal = False
        o.vc = None
        deps = {}
        for k in o.reads:
            w = self.last_w.get(k)
            if w is not None:
                deps[(w.idx, "raw")] = w
        for k in o.writes:
            w = self.last_w.get(k)
            if w is not None:
                deps[(w.idx, "waw")] = w
            for r in self.readers.get(k, ()):
                deps[(r.idx, "war")] = r
        for k in o.reads + o.writes:
            if isinstance(k, tuple) and k[0] == "ps":
                la = self.last_acc.setdefault(k, {})
                for e2, d in la.items():
                    if e2 != eng:
                        deps[(d.idx, "x")] = d
                la[eng] = o
        real = {}
        for (di, kind), d in deps.items():
            if d is o:
                continue
            if (not d.dma) and (not o.dma) and d.eng == o.eng:
                if o.eng == "pe":
                    continue
            real[di] = d
        o.deps = list(real.values())
        for d in o.deps:
            d.signal = True
        for k in o.reads:
            self.readers.setdefault(k, []).append(o)
        for k in o.writes:
            self.last_w[k] = o
            self.readers[k] = []
        self.ops.append(o)
        return o

    def emit(self, final_wait_ops=()):
        nc = self.nc
        esem = {e: nc.alloc_semaphore("s_" + e) for e in self.engs}
        ecount = {e: 0 for e in self.engs}
        dsem, dcount = {}, {}
        for o in final_wait_ops:
            o.signal = True
        for o in self.ops:
            if o.dma:
                if o.dkey not in dsem:
                    dsem[o.dkey] = nc.alloc_semaphore("d_%d" % (len(dsem),))
                    dcount[o.dkey] = 0
                dcount[o.dkey] += 16 * o.ndma
                o.sem, o.val = dsem[o.dkey], dcount[o.dkey]
            elif o.signal:
                ecount[o.eng] += 1
                o.sem, o.val = esem[o.eng], ecount[o.eng]
            else:
                o.sem, o.val = None, None
        known = {e: {} for e in self.engs}
        nwait = 0
        for o in self.ops:
            kn = known[o.eng]
            eng = self.engs[o.eng]
            need = {}
            for d in o.deps:
                sid = id(d.sem)
                if kn.get(sid, 0) >= d.val:
                    continue
                if sid not in need or need[sid][1] < d.val:
                    need[sid] = (d.sem, d.val, d)
            for sid, (sem, val, d) in sorted(need.items(), key=lambda t: -t[1][2].idx):
                if kn.get(sid, 0) >= val:
                    continue
                eng.wait_ge(sem, val)
                nwait += 1
                for s2, v2 in d.vc.items():
                    if kn.get(s2, 0) < v2:
                        kn[s2] = v2
            res = o.fn(eng)
            if o.dma:
                insts = res if isinstance(res, (list, tuple)) else [res]
                assert len(insts) == o.ndma, (len(insts), o.ndma)
                for i in insts:
                    i.then_inc(o.sem, 16)
                o.vc = dict(kn)
                o.vc[id(o.sem)] = o.val
            elif o.signal:
                res.then_inc(o.sem, 1)
                o.vc = dict(kn)
                o.vc[id(o.sem)] = o.val
        for o in final_wait_ops:
            self.engs["sp"].wait_ge(o.sem, o.val)
        for k, sem in dsem.items():
            self.engs["sp"].wait_ge(sem, dcount[k])
        self.nwait = nwait


def perm_att():
    idx = np.zeros((4, 128), np.int64)
    for c in range(4):
        for p in range(128):
            idx[c, p] = (4 * (p // 64) + c) * 64 + p % 64
    return idx


def vec_layout():
    names = [("mix_g", 8), ("ffn_g", 8), ("qg", 1), ("kg", 1), ("aog", 4), ("cvw", 62),
             ("cvb", 2), ("lng", 2), ("lnb", 2), ("cog", 2), ("gng", 2), ("fw", 132), ("fb", 44)]
    off, o = {}, 0
    for n, c in names:
        off[n] = o
        o += c
    return off, o


VOFF, NV = vec_layout()
A_A, A_MC, A_MP, A_OM, NALB = 1536, 1664, 1792, 1920, 1984
C_ATTD, C_DEC2, C_XI, C_ZETA, C_CDEC, C_ID, NCST = 0, 640, 1152, 1408, 1664, 1666, 1794


def unit_schedule():
    sch = []
    for nm in ("v", "rv0", "rv1", "rk0", "rk1"):
        sch.append(("tm", nm))
    for j in range(2):
        sch.append(("in", "ca%d" % j))
        sch.append(("in", "cb%d" % j))
    sch.append(("pw",))
    sch.append(("wo_cv", 0))
    sch.append(("wo_cv", 1))
    for nm in ("rq0", "rq1", "rkf0", "rkf1"):
        sch.append(("in", nm))
    sch.append(("in", "rg0"))
    sch.append(("in", "rg1"))
    sch.append(("wo_ret", 0))
    sch.append(("wo_ret", 1))
    for nm in ("q0", "q1", "q2", "q3", "k"):
        sch.append(("in", nm))
    for a in range(4):
        sch.append(("wo_att", a))
    for pi, part in enumerate(PARTS):
        for c in part:
            sch.append(("up", c, 0))
            sch.append(("up", c, 1))
        for m in range(8):
            sch.append(("down", pi, m))
    return sch


def in_cols(nm):
    pa = perm_att()
    base = {"k": 512, "v": 640, "ca": 768, "cb": 1024, "rq": 1280, "rk": 1536, "rkf": 1536,
            "rv": 1792, "rg": 2048}
    if nm[0] == "q" and len(nm) == 2:
        return pa[int(nm[1])]
    if nm in ("k", "v"):
        return base[nm] + np.arange(128)
    b = nm[:-1]
    return base[b] + int(nm[-1]) * 128 + np.arange(128)


def kc_unit(W, cols):
    return W[:, cols].reshape(8, 128, len(cols)).transpose(1, 0, 2).reshape(128, -1)


def pack_units(w_in, cv_pw, w_out, ffn_up, ffn_down):
    pa = perm_att()
    sch = unit_schedule()
    out = np.zeros((DEPTH * len(sch), 128, 1024), np.float32)
    for l in range(DEPTH):
        for ui, u in enumerate(sch):
            dst = out[l * len(sch) + ui]
            if u[0] in ("tm", "in"):
                dst[:, :] = kc_unit(w_in[l], in_cols(u[1]))
            elif u[0] == "pw":
                dst[:, :512] = cv_pw[l].reshape(2, 128, 256).transpose(1, 0, 2).reshape(128, 512)
            elif u[0] == "wo_att":
                a = u[1]
                blk = np.zeros((128, 2, 4, 128), np.float32)
                for ml in range(2):
                    for kc in range(4):
                        blk[:, ml, kc, :] = w_out[l][pa[kc], (2 * a + ml) * 128:(2 * a + ml + 1) * 128]
                dst[:, :] = blk.reshape(128, 1024)
            elif u[0] in ("wo_cv", "wo_ret"):
                b = u[1]
                r0 = 512 if u[0] == "wo_cv" else 768
                blk = np.zeros((128, 4, 2, 128), np.float32)
                for ml in range(4):
                    for kc in range(2):
                        blk[:, ml, kc, :] = w_out[l][r0 + kc * 128:r0 + (kc + 1) * 128,
                                                     (4 * b + ml) * 128:(4 * b + ml + 1) * 128]
                dst[:, :] = blk.reshape(128, 1024)
            elif u[0] == "up":
                c, hu = u[1], u[2]
                dst[:, :] = kc_unit(ffn_up[l], hu * DFF + c * 128 + np.arange(128))
            elif u[0] == "down":
                cs, m = PARTS[u[1]], u[2]
                blk = np.zeros((128, 8, 128), np.float32)
                for ci, c in enumerate(cs):
                    blk[:, ci, :] = ffn_down[l][c * 128:(c + 1) * 128, m * 128:(m + 1) * 128]
                dst[:, :] = blk.reshape(128, 1024)
    return out


def pack_vecs(inp):
    pa = perm_att()
    v = np.zeros((128, DEPTH * NV), np.float32)
    p = np.arange(128)
    for l in range(DEPTH):
        o = l * NV
        v[:, o + VOFF["mix_g"]:o + VOFF["mix_g"] + 8] = inp["norm_mix_g"][l].reshape(8, 128).T
        v[:, o + VOFF["ffn_g"]:o + VOFF["ffn_g"] + 8] = inp["norm_ffn_g"][l].reshape(8, 128).T
        v[:, o + VOFF["qg"]] = inp["q_norm_g"][l][p % 64]
        v[:, o + VOFF["kg"]] = inp["k_norm_g"][l][p % 64]
        for c in range(4):
            v[:, o + VOFF["aog"] + c] = inp["attn_out_g"][l][pa[c]]
        for j in range(2):
            v[:, o + VOFF["cvw"] + j * 31:o + VOFF["cvw"] + (j + 1) * 31] = \
                inp["cv_dw_w"][l][:, j * 128:(j + 1) * 128].T
            v[:, o + VOFF["cvb"] + j] = inp["cv_dw_b"][l][j * 128:(j + 1) * 128]
            v[:, o + VOFF["lng"] + j] = inp["cv_ln_g"][l][j * 128:(j + 1) * 128]
            v[:, o + VOFF["lnb"] + j] = inp["cv_ln_b"][l][j * 128:(j + 1) * 128]
            v[:, o + VOFF["cog"] + j] = inp["cv_out_g"][l][j * 128:(j + 1) * 128]
            v[:, o + VOFF["gng"] + j] = inp["ret_gn_g"][l][j * 128:(j + 1) * 128]
        for c in range(NCF):
            for hu in range(2):
                u = 2 * c + hu
                col = hu * DFF + c * 128
                v[:, o + VOFF["fw"] + u * 3:o + VOFF["fw"] + u * 3 + 3] = \
                    inp["ffn_dw_w"][l][:, col:col + 128].T
                v[:, o + VOFF["fb"] + u] = inp["ffn_dw_b"][l][col:col + 128]
    return v


def const_tables():
    c = np.zeros((128, NCST), np.float64)
    j = np.arange(128)[:, None].astype(np.float64)
    i = np.arange(128)[None, :].astype(np.float64)
    attd = np.zeros((128, 5, 128))
    attd[:, 0, :] = np.where(i >= j, i - j, BIG)
    attd[:, 1, :] = np.where(j > i, 128 + i - j, BIG)
    attd[:, 2, :] = np.where(j >= 112, 128.0, BIG) + 0 * i
    attd[:, 3, :] = np.where(j >= 112, np.minimum(128 + i - j, 128.0), BIG)
    attd[:, 4, :] = np.where((j >= 112) & (i >= j), i - j, BIG)
    c[:, C_ATTD:C_ATTD + 640] = attd.reshape(128, 640)
    gam = np.array([1.0 - 2.0 ** (-5.0 - h) for h in range(4)])
    dec2 = np.zeros((128, 4, 128))
    for h in range(4):
        dec2[:, HPOS[h], :] = np.where(i >= j, gam[h] ** (-(j + 1.0)) / 8.0, 0.0)
    c[:, C_DEC2:C_DEC2 + 512] = dec2.reshape(128, 512)
    xi = np.zeros((128, 2, 128))
    cdec = np.zeros((128, 2))
    for p in range(128):
        for cc in range(2):
            h = 2 * cc + p // 64
            xi[p, cc, :] = gam[h] ** (np.arange(128) + 1.0)
            cdec[p, cc] = gam[h] ** 128.0
    c[:, C_XI:C_XI + 256] = xi.reshape(128, 256)
    zeta = np.zeros((128, 256))
    for h in range(4):
        zeta[:, h * 64:(h + 1) * 64] = (gam[h] ** (127.0 - np.arange(128)) / 8.0)[:, None]
    c[:, C_ZETA:C_ZETA + 256] = zeta
    c[:, C_CDEC:C_CDEC + 2] = cdec
    c[:, C_ID:C_ID + 128] = np.eye(128)
    return c.astype(np.float32)


def alibi_tables():
    a = np.zeros((128, NALB), np.float64)
    i = np.arange(128, dtype=np.float64)
    for g in range(2):
        pb = 64 * g
        for ty, off in ((0, 0.0), (1, 128.0)):
            for c in range(4):
                sl = SLOPES[4 * g + c]
                a[pb, ty * 512 + c * 128:ty * 512 + (c + 1) * 128] = -sl * (i + off)
                a[pb + 1, ty * 512 + c * 128:ty * 512 + (c + 1) * 128] = sl
        for c in range(4):
            a[pb, 2 * 512 + c * 128:2 * 512 + (c + 1) * 128] = -SLOPES[4 * g + c] * 128.0
        a[pb, A_A:A_A + 128] = 1.0
        a[pb + 1, A_A:A_A + 128] = i
    jj = np.arange(128)[:, None]
    ii = np.arange(128)[None, :]
    a[:, A_MC:A_MC + 128] = (ii >= jj)
    a[:, A_MP:A_MP + 128] = (jj > ii)
    a[112:128, A_OM:A_OM + 64] = 1.0
    return a.astype(np.float32)


class Ctx:
    pass


def tile_gi(t):
    return 0 if t == 0 else 1 + (t - 1) // 4


class _Stop(Exception):
    pass


def build_program(depth=DEPTH, dumps=(), stop=None):
    nc = bass.Bass("TRN2", target_bir_lowering=False)
    P = Prog(nc)
    cx = Ctx()
    cx.nc, cx.P = nc, P
    nunit_l = len(unit_schedule())
    NU = depth * nunit_l
    xT = nc.dram_tensor("xT", [D, S], F32, kind="ExternalInput").ap()
    metaT = nc.dram_tensor("metaT", [D, 16], F32, kind="ExternalInput").ap()
    wts = nc.dram_tensor("wts", [DEPTH * nunit_l, 128, 1024], F32, kind="ExternalInput").ap()
    vec_d = nc.dram_tensor("vec", [128, DEPTH * NV], F32, kind="ExternalInput").ap()
    cst_d = nc.dram_tensor("cst", [128, NCST], F32, kind="ExternalInput").ap()
    snk_d = nc.dram_tensor("snk", [DEPTH, 8], F32, kind="ExternalInput").ap()
    alb_d = nc.dram_tensor("alb", [128, NALB], F32, kind="ExternalInput").ap()
    outT = nc.dram_tensor("outT", [D, S], F32, kind="ExternalOutput").ap()
    hT = nc.alloc_sbuf_tensor("hT", [128, 8, T], F32)
    uT = nc.alloc_sbuf_tensor("uT", [128, 8, T], BF16)
    AR = nc.alloc_sbuf_tensor("AR", [128, 10, T], BF16)
    DG = nc.alloc_sbuf_tensor("DG", [128, 6, 128], BF16)
    W8 = nc.alloc_sbuf_tensor("W8", [128, NB, 1024], BF16)
    SF = nc.alloc_sbuf_tensor("SF", [128, NSF, 512], F32)
    SB = nc.alloc_sbuf_tensor("SB", [128, NSB, 516], BF16)
    CST = nc.alloc_sbuf_tensor("CST", [128, NCST], F32)
    VEC = nc.alloc_sbuf_tensor("VEC", [128, DEPTH * NV], F32)
    EPS = nc.alloc_sbuf_tensor("EPS", [128, 2], F32)
    GQ8 = nc.alloc_sbuf_tensor("GQ8", [128, DEPTH], F32)
    ESK = nc.alloc_sbuf_tensor("ESK", [128, DEPTH * 4], F32)
    ALB = nc.alloc_sbuf_tensor("ALB", [128, NALB], BF16)
    ONES = nc.alloc_sbuf_tensor("ONES", [128, 128], BF16)
    BONES = nc.alloc_sbuf_tensor("BONES", [128, 128], BF16)
    RF = nc.alloc_sbuf_tensor("RF", [128, 2, 64], F32)
    PS = nc.alloc_psum_tensor("PS", [128, 8, 512], F32)
    cx.psi = cx.sfi = cx.sbi = 0
    cx.u_issued = 0
    cx.u_cons = 0
    cx.dump_outs = []

    def ps_alloc():
        b = cx.psi % 8
        cx.psi += 1
        return b

    def run_pipe(gens, k):
        gens = list(gens)
        nxt, active = 0, []
        while nxt < len(gens) or active:
            keep = []
            for g in active:
                try:
                    next(g)
                    keep.append(g)
                except StopIteration:
                    pass
            active = keep
            if nxt < len(gens) and len(active) < k:
                g = gens[nxt]
                nxt += 1
                try:
                    next(g)
                    active.append(g)
                except StopIteration:
                    pass

    def ring(items):
        st = [0]

        def f():
            v = items[st[0] % len(items)]
            st[0] += 1
            return v
        return f

    def sf_alloc():
        b = cx.sfi % NSF
        cx.sfi += 1
        return b

    def sb_alloc():
        b = cx.sbi % NSB
        cx.sbi += 1
        return b

    def MM(out, lhsT, rhs, start, stop, reads, writes):
        P.add("pe", lambda e: e.matmul(out, lhsT=lhsT, rhs=rhs, start=start, stop=stop), reads, writes)

    def ACT(out, in_, func, reads, writes, scale=1.0, bias=None):
        if bias is None:
            P.add("act", lambda e: e.activation(out=out, in_=in_, func=func, scale=scale), reads, writes)
        else:
            P.add("act", lambda e: e.activation(out=out, in_=in_, func=func, scale=scale, bias=bias),
                  reads, writes)

    def TT(eng, out, in0, in1, op, reads, writes):
        P.add(eng, lambda e: e.tensor_tensor(out=out, in0=in0, in1=in1, op=op), reads, writes)

    def TS(eng, out, in0, s1, s2, op0, op1, reads, writes):
        if s2 is None:
            P.add(eng, lambda e: e.tensor_scalar(out=out, in0=in0, scalar1=s1, scalar2=None, op0=op0),
                  reads, writes)
        else:
            P.add(eng, lambda e: e.tensor_scalar(out=out, in0=in0, scalar1=s1, scalar2=s2, op0=op0, op1=op1),
                  reads, writes)

    def STT(out, in0, scalar, in1, op0, op1, reads, writes):
        P.add("dve", lambda e: e.scalar_tensor_tensor(out=out, in0=in0, scalar=scalar, in1=in1,
                                                      op0=op0, op1=op1), reads, writes)

    def RECIP(out, in_, reads, writes):
        P.add("dve", lambda e: e.reciprocal(out=out, in_=in_), reads, writes)

    def MEMSET(eng, ap, val, writes):
        P.add(eng, lambda e: e.memset(ap, val), (), writes)

    def DMA(eng, out, in_, reads, writes, dkey):
        return P.add(eng, lambda e: e.dma_start(out=out, in_=in_), reads, writes, dma=True, dkey=dkey)

    def stage(name):
        if stop == name:
            raise _Stop()

    def dump(name, ap, shape, dtype, reads):
        if name not in dumps:
            return
        dt = nc.dram_tensor("dbg_" + name, list(shape), dtype, kind="ExternalOutput").ap()
        o = P.add("sp", lambda e: e.dma_start(out=dt, in_=ap), reads, (), dma=True, dkey=("dbg", name))
        cx.dump_outs.append(o)

    def HK(c, gi):
        return ("h", c, gi)

    def UK(c, gi):
        return ("u", c, gi)

    def AK(slot, lo, n):
        return [("a", slot, t) for t in range(lo // 128, (lo + n + 127) // 128)]

    def AKF(slot0, flo, fn):
        ks = []
        a = flo
        while a < flo + fn:
            s = slot0 + a // T
            col = a % T
            ks.append(("a", s, col // 128))
            a = (a // 128 + 1) * 128
        return ks

    def A(slot, lo, n):
        return AR[:, slot, lo:lo + n]

    def vcol(l, name, i=0):
        o = l * NV + VOFF[name] + i
        return VEC[:, o:o + 1]

    def use_units(k):
        u0 = cx.u_cons
        lim = min(NU - 1, u0 - 1 + NB)
        while cx.u_issued <= lim:
            j = cx.u_issued
            s = j % NB
            P.add("pool", (lambda jj, ss: (lambda e: e.dma_start(out=W8[:, ss, :], in_=wts[jj, :, :])))(j, s),
                  (), [("w", s)], dma=True, dkey=("w", s))
            cx.u_issued += 1
        cx.u_cons += k
        assert cx.u_issued >= cx.u_cons
        return [(u0 + i) % NB for i in range(k)]

    o_c = P.add("sp", lambda e: [e.dma_start(out=CST[:, :], in_=cst_d[:, :]),
                                 e.dma_start(out=VEC[:, :], in_=vec_d[:, :])],
                (), ["cst", "vec"], dma=True, dkey="cst", ndma=2)

    def snk_load(e):
        r = []
        for l in range(DEPTH):
            for g in range(2):
                r.append(e.dma_start(out=ESK[64 * g:64 * g + 64, 4 * l:4 * l + 4],
                                     in_=snk_d[l:l + 1, 4 * g:4 * g + 4].partition_broadcast(64)))
        return r
    P.add("sp", snk_load, (), ["esk"], dma=True, dkey="snk", ndma=2 * DEPTH)
    P.add("pool", lambda e: e.dma_start(out=ALB[:, :], in_=alb_d[:, :]), (), ["alb"], dma=True, dkey="alb")
    for c in range(8):
        P.add("sp", (lambda cc: (lambda e: [e.dma_start(out=hT[:, cc, 128:T], in_=xT[cc * 128:(cc + 1) * 128, :]),
                                            e.dma_start(out=hT[:, cc, 112:128], in_=metaT[cc * 128:(cc + 1) * 128, :])]))(c),
              (), [HK(c, gi) for gi in range(5)], dma=True, dkey=("hin", c), ndma=2)
    MEMSET("pool", EPS[:, 0:1], RMS_EPS, ["eps"])
    MEMSET("pool", EPS[:, 1:2], LN_EPS, ["eps"])
    MEMSET("pool", ONES[:, :], 1.0, ["ones"])
    MEMSET("pool", BONES[:, :], 0.0, ["bones"])
    MEMSET("pool", BONES[0:64, 0:64], 1.0, ["bones"])
    MEMSET("pool", BONES[64:128, 64:128], 1.0, ["bones"])
    for c in range(8):
        MEMSET("dve", hT[:, c, 0:112], 0.0, [("hpad", c)])
        MEMSET("pool", uT[:, c, 0:112], 0.0, [UK(c, 0)])
    for sl in range(10):
        MEMSET("pool", AR[:, sl, :], 0.0, AK(sl, 0, T))
    for i in range(NSF):
        MEMSET("pool", SF[:, i, :], 0.0, [("sf", i)])
    for i in range(NSB):
        MEMSET("pool", SB[:, i, :], 0.0, [("sb", i), ("sbh", i)])
    ACT(ESK[:, :], ESK[:, :], AF.Exp, ["esk"], ["esk"])
    for l in range(depth):
        TS("dve", GQ8[:, l:l + 1], vcol(l, "qg"), 0.125, None, ALU.mult, None, ["vec"], ["gq8"])

    def rstd_from_ps(b, n, inv_n, eps_i, extra_reads=()):
        i = sf_alloc()
        ACT(SF[:, i, 0:n], PS[:, b, 0:n], AF.Ln, [("ps", b), "eps"] + list(extra_reads), [("sf", i)],
            scale=inv_n, bias=EPS[:, eps_i:eps_i + 1])
        ACT(SF[:, i, 0:n], SF[:, i, 0:n], AF.Exp, [("sf", i)], [("sf", i)], scale=-0.5)
        return i

    def rmsnorm_main(l, gname):
        def it(gi, lo, n):
            b = ps_alloc()
            for c in range(8):
                s = sb_alloc()
                ACT(SB[:, s, 0:n], hT[:, c, lo:lo + n], AF.Square, [HK(c, gi), ("hpad", c)], [("sb", s)])
                MM(PS[:, b, 0:n], ONES[:, :], SB[:, s, 0:n], c == 0, c == 7, [("sb", s), "ones"], [("ps", b)])
            yield
            r = rstd_from_ps(b, n, 1.0 / D, 0)
            yield
            lo2, n2 = (112, 16) if gi == 0 else (lo, n)
            off = lo2 - lo
            for c in range(8):
                STT(uT[:, c, lo2:lo2 + n2], hT[:, c, lo2:lo2 + n2], vcol(l, gname, c), SF[:, r, off:off + n2],
                    ALU.mult, ALU.mult, [HK(c, gi), ("sf", r), "vec"], [UK(c, gi)])
        run_pipe([it(gi, lo, n) for gi, (lo, n) in enumerate(GRP)], 2)

    def fm_proj(s, gi, b, grp=GRP):
        lo, n = grp[gi]
        for kc in range(8):
            MM(PS[:, b, 0:n], W8[:, s, kc * 128:(kc + 1) * 128], uT[:, kc, lo:lo + n], kc == 0, kc == 7,
               [("w", s), UK(kc, gi)], [("ps", b)])

    def norm_full(l, slots, gname, nfeat):
        def it(gi, lo, n):
            b = ps_alloc()
            for ci, sl in enumerate(slots):
                s = sb_alloc()
                ACT(SB[:, s, 0:n], A(sl, lo, n), AF.Square, AK(sl, lo, n), [("sb", s)])
                MM(PS[:, b, 0:n], ONES[:, :], SB[:, s, 0:n], ci == 0, ci == len(slots) - 1,
                   [("sb", s), "ones"], [("ps", b)])
            yield
            r = rstd_from_ps(b, n, 1.0 / nfeat, 0)
            yield
            for ci, sl in enumerate(slots):
                STT(A(sl, lo, n), A(sl, lo, n), vcol(l, gname, ci), SF[:, r, 0:n], ALU.mult, ALU.mult,
                    AK(sl, lo, n) + [("sf", r), "vec"], AK(sl, lo, n))
        run_pipe([it(gi, lo, n) for gi, (lo, n) in enumerate(GRP)], 2)

    def wout_partial(l, kind, yslots, group_outer=False):
        nk = len(yslots)
        mper = 4 if nk == 2 else 2
        nun = 8 // mper

        def one(s, ml, m, gi, lo, n):
            b = ps_alloc()
            for kc in range(nk):
                col = (ml * nk + kc) * 128
                MM(PS[:, b, 0:n], W8[:, s, col:col + 128], A(yslots[kc], lo, n), kc == 0, kc == nk - 1,
                   [("w", s)] + AK(yslots[kc], lo, n), [("ps", b)])
            TT("dve", hT[:, m, lo:lo + n], hT[:, m, lo:lo + n], PS[:, b, 0:n], ALU.add,
               [HK(m, gi), ("ps", b)], [HK(m, gi)])
        if group_outer:
            su_ = use_units(nun)
            for gi, (lo, n) in enumerate(GRP16):
                for ub in range(nun):
                    for ml in range(mper):
                        one(su_[ub], ml, ub * mper + ml, gi, lo, n)
            return
        for ub in range(nun):
            (s,) = use_units(1)
            for ml in range(mper):
                for gi, (lo, n) in enumerate(GRP16):
                    one(s, ml, ub * mper + ml, gi, lo, n)

    CD = lambda off, n: CST[:, off:off + n]
    out_ops = []
    try:
      stage("pro")
      for l in range(depth):
          last = (l == depth - 1)
          rmsnorm_main(l, "mix_g")
          dump("u%d" % l, uT[:, :, :], [128, 8, T], BF16, [UK(c, gi) for c in range(8) for gi in range(5)])
          stage("norm1")
          su = use_units(5)
          stage("tm0")
          AR01 = AR[:, 0:2, :].rearrange("p a b -> p (a b)")
          AR23 = AR[:, 2:4, :].rearrange("p a b -> p (a b)")
          for t in range(NT):
              stage("tmt%d" % t)
              gi = tile_gi(t)
              b0 = ps_alloc()
              b1 = ps_alloc()
              for ui in range(5):
                  s = su[ui]
                  dst = PS[:, b0, ui * 128:(ui + 1) * 128] if ui < 4 else PS[:, b1, 0:128]
                  bb = b0 if ui < 4 else b1
                  for kc in range(8):
                      MM(dst, uT[:, kc, t * 128:(t + 1) * 128], W8[:, s, kc * 128:(kc + 1) * 128], kc == 0, kc == 7,
                         [("w", s), UK(kc, gi)], [("ps", bb)])
              ACT(A(9, t * 128, 128), PS[:, b0, 0:128], AF.Copy, [("ps", b0)], AK(9, t * 128, 128))
              ACT(AR01[:, t * 256:(t + 1) * 256], PS[:, b0, 128:384], AF.Copy, [("ps", b0)], AKF(0, t * 256, 256))
              TT("dve", AR23[:, t * 256:t * 256 + 128], PS[:, b0, 384:512], CD(C_ZETA, 128), ALU.mult,
                 [("ps", b0), "cst"], AKF(2, t * 256, 128))
              TT("dve", AR23[:, t * 256 + 128:t * 256 + 256], PS[:, b1, 0:128], CD(C_ZETA + 128, 128), ALU.mult,
                 [("ps", b1), "cst"], AKF(2, t * 256 + 128, 128))
          dump("vtm%d" % l, AR[:, 9, :], [128, T], BF16, AK(9, 0, T))
          dump("rvtm%d" % l, AR01, [128, 2 * T], BF16, AK(0, 0, T) + AK(1, 0, T))
          dump("kztm%d" % l, AR23, [128, 2 * T], BF16, AK(2, 0, T) + AK(3, 0, T))
          stage("tm")

          for j in range(2):
              sa, sbw = use_units(2)
              for gi, (lo, n) in enumerate(GRP):
                  ba, bb = ps_alloc(), ps_alloc()
                  fm_proj(sa, gi, ba)
                  fm_proj(sbw, gi, bb)
                  i = sf_alloc()
                  ACT(SF[:, i, 0:n], PS[:, bb, 0:n], AF.Sigmoid, [("ps", bb)], [("sf", i)])
                  TT("dve", A(4 + j, lo, n), PS[:, ba, 0:n], SF[:, i, 0:n], ALU.mult,
                     [("ps", ba), ("sf", i)], AK(4 + j, lo, n))
          stage("conv1")
          ps_scan = ring([5, 6, 7])

          def scan_gen():
              MEMSET("dve", RF[:, :, :], 0.0, ["rf"])
              for t in range(NT - 1):
                  b = ps_scan()
                  for h in range(4):
                      pb, cc = 64 * (h % 2), h // 2
                      MM(PS[pb:pb + 64, b, cc * 64:(cc + 1) * 64],
                         AR23[:, t * 256 + h * 64:t * 256 + (h + 1) * 64],
                         AR01[:, t * 256 + h * 64:t * 256 + (h + 1) * 64], True, True,
                         AKF(2, t * 256 + h * 64, 64) + AKF(0, t * 256 + h * 64, 64), [("ps", b)])
                  for cc in range(2):
                      STT(RF[:, cc, :], RF[:, cc, :], CST[:, C_CDEC + cc:C_CDEC + cc + 1], PS[:, b, cc * 64:(cc + 1) * 64],
                          ALU.mult, ALU.add, ["rf", ("ps", b), "cst"], ["rf"])
                  ACT(A(8, (t + 1) * 128, 128), RF[:, :, :].rearrange("p a b -> p (a b)"), AF.Copy, ["rf"],
                      AK(8, (t + 1) * 128, 128))
                  yield
          scan = scan_gen()
          CG = [(112, 16)] + GRP[1:]
          dg_ring = ring(list(range(6)))
          for j in range(2):
              wc = l * NV + VOFF["cvw"] + j * 31
              banks = [0, 1, 2, 3, 4]
              dgs = {}

              def build_dg(k):
                  d = dg_ring()
                  dgs[k] = d
                  if k % 2 == 0:
                      TS("dve", DG[:, d, :], CST[:, C_ID:C_ID + 128], VEC[:, wc + k:wc + k + 1], None, ALU.mult, None,
                         ["cst", "vec"], [("dg", d)])
                  else:
                      ACT(DG[:, d, :], CST[:, C_ID:C_ID + 128], AF.Copy, ["cst", "vec"], [("dg", d)],
                          scale=VEC[:, wc + k:wc + k + 1])
              for k in range(3):
                  build_dg(k)
              for k in range(31):
                  sh = 30 - k
                  if k % 4 == 1:
                      next(scan, None)
                  if k + 3 < 31:
                      build_dg(k + 3)
                  d = dgs[k]
                  for gi, (lo, n) in enumerate(CG):
                      MM(PS[:, banks[gi], 0:n], DG[:, d, :], A(4 + j, lo - sh, n), k == 0, k == 30,
                         [("dg", d)] + AK(4 + j, lo - sh, n), [("ps", banks[gi])])
              for gi, (lo, n) in enumerate(CG):
                  ACT(A(6 + j, lo, n), PS[:, banks[gi], 0:n], AF.Identity, [("ps", banks[gi]), "vec"], AK(6 + j, lo, n),
                      bias=vcol(l, "cvb", j))
          dump("cvo%d" % l, AR[:, 6:8, :], [128, 2, T], BF16, AK(6, 0, T) + AK(7, 0, T))
          stage("conv2")
          for _ in scan:
              pass
          def cln_it(gi, lo, n):
              bm, bs = ps_alloc(), ps_alloc()
              for j in range(2):
                  s = sb_alloc()
                  ACT(SB[:, s, 0:n], A(6 + j, lo, n), AF.Square, AK(6 + j, lo, n), [("sb", s)])
                  MM(PS[:, bm, 0:n], ONES[:, :], A(6 + j, lo, n), j == 0, j == 1, AK(6 + j, lo, n) + ["ones"], [("ps", bm)])
                  MM(PS[:, bs, 0:n], ONES[:, :], SB[:, s, 0:n], j == 0, j == 1, [("sb", s), "ones"], [("ps", bs)])
              i1 = sf_alloc()
              i2 = sf_alloc()
              yield
              ACT(SF[:, i1, 0:n], PS[:, bm, 0:n], AF.Identity, [("ps", bm)], [("sf", i1)], scale=-1.0 / 256)
              yield
              TT("dve", SF[:, i2, 0:n], SF[:, i1, 0:n], SF[:, i1, 0:n], ALU.mult, [("sf", i1)], [("sf", i2)])
              STT(SF[:, i2, 0:n], PS[:, bs, 0:n], 1.0 / 256, SF[:, i2, 0:n], ALU.mult, ALU.subtract,
                  [("ps", bs), ("sf", i2)], [("sf", i2)])
              i3s = [sf_alloc(), sf_alloc()]
              for j in range(2):
                  TT("dve", SF[:, i3s[j], 0:n], A(6 + j, lo, n), SF[:, i1, 0:n], ALU.add,
                     AK(6 + j, lo, n) + [("sf", i1)], [("sf", i3s[j])])
              yield
              ACT(SF[:, i2, 0:n], SF[:, i2, 0:n], AF.Ln, [("sf", i2), "eps"], [("sf", i2)], bias=EPS[:, 1:2])
              ACT(SF[:, i2, 0:n], SF[:, i2, 0:n], AF.Exp, [("sf", i2)], [("sf", i2)], scale=-0.5)
              yield
              for j in range(2):
                  TT("dve", SF[:, i3s[j], 0:n], SF[:, i3s[j], 0:n], SF[:, i2, 0:n], ALU.mult,
                     [("sf", i3s[j]), ("sf", i2)], [("sf", i3s[j])])
              yield
              for j in range(2):
                  ACT(A(4 + j, lo, n), SF[:, i3s[j], 0:n], AF.Silu, [("sf", i3s[j]), "vec"], AK(4 + j, lo, n),
                      scale=vcol(l, "lng", j), bias=vcol(l, "lnb", j))
          run_pipe([cln_it(gi, lo, n) for gi, (lo, n) in enumerate(CG)], 2)
          (spw,) = use_units(1)
          for m in range(2):
              for gi, (lo, n) in enumerate(CG):
                  b = ps_alloc()
                  for kc in range(2):
                      col = (kc * 2 + m) * 128
                      MM(PS[:, b, 0:n], W8[:, spw, col:col + 128], A(4 + kc, lo, n), kc == 0, kc == 1,
                         [("w", spw)] + AK(4 + kc, lo, n), [("ps", b)])
                  ACT(A(6 + m, lo, n), PS[:, b, 0:n], AF.Copy, [("ps", b)], AK(6 + m, lo, n))
          norm_full(l, [6, 7], "cog", 256)
          dump("ycv%d" % l, AR[:, 6:8, :], [128, 2, T], BF16, AK(6, 0, T) + AK(7, 0, T))
          stage("cv")
          wout_partial(l, "cv", [6, 7])

          su = use_units(4)
          for c in range(2):
              for gi, (lo, n) in enumerate(GRP):
                  b = ps_alloc()
                  fm_proj(su[c], gi, b)
                  ntl = n // 128
                  TT("dve", A(4 + c, lo, n).rearrange("p (a b) -> p a b", b=128),
                     PS[:, b, 0:n].rearrange("p (a b) -> p a b", b=128),
                     CST[:, C_XI + c * 128:C_XI + (c + 1) * 128].unsqueeze(1).to_broadcast([128, ntl, 128]),
                     ALU.mult, [("ps", b), "cst"], AK(4 + c, lo, n))
          for c in range(2):
              for gi, (lo, n) in enumerate(GRP):
                  b = ps_alloc()
                  fm_proj(su[2 + c], gi, b)
                  ACT(A(6 + c, lo, n), PS[:, b, 0:n], AF.Copy, [("ps", b)], AK(6 + c, lo, n))
          stage("ret1")
          stage("ret2")
          ps_rs = ring([0, 1, 2, 3, 4, 5])
          ps_ry = ring([6, 7])
          def ret_it(t):
              bSe, bSo = ps_rs(), ps_rs()
              for h in range(4):
                  pb, cc = 64 * (h % 2), h // 2
                  bS = bSe if h % 2 == 0 else bSo
                  MM(PS[:, bS, cc * 128:(cc + 1) * 128], AR[pb:pb + 64, 6 + cc, t * 128:(t + 1) * 128],
                     AR[pb:pb + 64, 4 + cc, t * 128:(t + 1) * 128], True, True,
                     AK(6 + cc, t * 128, 128) + AK(4 + cc, t * 128, 128), [("ps", bS)])
              yield
              s = sb_alloc()
              TT("dve", SB[:, s, 0:256], PS[:, bSe, 0:256], CD(C_DEC2, 256), ALU.mult, [("ps", bSe), "cst"], [("sb", s)])
              TT("dve", SB[:, s, 256:512], PS[:, bSo, 0:256], CD(C_DEC2 + 256, 256), ALU.mult,
                 [("ps", bSo), "cst"], [("sb", s)])
              yield
              bY = ps_ry()
              for h in range(4):
                  pb, cc = 64 * (h % 2), h // 2
                  MM(PS[pb:pb + 64, bY, cc * 128:(cc + 1) * 128], AR01[:, t * 256 + h * 64:t * 256 + (h + 1) * 64],
                     SB[:, s, HPOS[h] * 128:(HPOS[h] + 1) * 128], True, t == 0,
                     AKF(0, t * 256 + h * 64, 64) + [("sb", s)], [("ps", bY)])
                  if t > 0:
                      MM(PS[pb:pb + 64, bY, cc * 128:(cc + 1) * 128],
                         AR[pb:pb + 64, 8, t * 128 + cc * 64:t * 128 + (cc + 1) * 64],
                         AR[pb:pb + 64, 4 + cc, t * 128:(t + 1) * 128], False, True,
                         AK(8, t * 128, 128) + AK(4 + cc, t * 128, 128), [("ps", bY)])
              yield
              ACT(AR[:, 2:4, t * 128:(t + 1) * 128], PS[:, bY, 0:256].rearrange("p (a b) -> p a b", b=128), AF.Copy,
                  [("ps", bY)], AK(2, t * 128, 128) + AK(3, t * 128, 128))
          run_pipe([ret_it(t) for t in range(NT)], 3)
          dump("yretraw%d" % l, AR[:, 2:4, :], [128, 2, T], BF16, AK(2, 0, T) + AK(3, 0, T))
          stage("ret3")
          srg = use_units(2)
          def gn_it(c, gi, lo, n):
              bm, bs, bg = ps_alloc(), ps_alloc(), ps_alloc()
              s = sb_alloc()
              ACT(SB[:, s, 0:n], A(2 + c, lo, n), AF.Square, AK(2 + c, lo, n), [("sb", s)])
              MM(PS[:, bm, 0:n], BONES[:, :], A(2 + c, lo, n), True, True, AK(2 + c, lo, n) + ["bones"], [("ps", bm)])
              MM(PS[:, bs, 0:n], BONES[:, :], SB[:, s, 0:n], True, True, [("sb", s), "bones"], [("ps", bs)])
              fm_proj(srg[c], gi, bg)
              i1, i2, i3, i4 = sf_alloc(), sf_alloc(), sf_alloc(), sf_alloc()
              yield
              ACT(SF[:, i1, 0:n], PS[:, bm, 0:n], AF.Identity, [("ps", bm)], [("sf", i1)], scale=-1.0 / 64)
              ACT(SF[:, i4, 0:n], PS[:, bg, 0:n], AF.Silu, [("ps", bg)], [("sf", i4)])
              yield
              TT("dve", SF[:, i2, 0:n], SF[:, i1, 0:n], SF[:, i1, 0:n], ALU.mult, [("sf", i1)], [("sf", i2)])
              STT(SF[:, i2, 0:n], PS[:, bs, 0:n], 1.0 / 64, SF[:, i2, 0:n], ALU.mult, ALU.subtract,
                  [("ps", bs), ("sf", i2)], [("sf", i2)])
              TT("dve", SF[:, i3, 0:n], A(2 + c, lo, n), SF[:, i1, 0:n], ALU.add,
                 AK(2 + c, lo, n) + [("sf", i1)], [("sf", i3)])
              yield
              ACT(SF[:, i2, 0:n], SF[:, i2, 0:n], AF.Ln, [("sf", i2), "eps"], [("sf", i2)], bias=EPS[:, 1:2])
              ACT(SF[:, i2, 0:n], SF[:, i2, 0:n], AF.Exp, [("sf", i2)], [("sf", i2)], scale=-0.5)
              yield
              STT(SF[:, i3, 0:n], SF[:, i3, 0:n], vcol(l, "gng", c), SF[:, i2, 0:n], ALU.mult, ALU.mult,
                  [("sf", i3), ("sf", i2), "vec"], [("sf", i3)])
              TT("dve", A(2 + c, lo, n), SF[:, i3, 0:n], SF[:, i4, 0:n], ALU.mult, [("sf", i3), ("sf", i4)],
                 AK(2 + c, lo, n))
          run_pipe([gn_it(c, gi, lo, n) for c in range(2) for gi, (lo, n) in enumerate(GRP)], 2)
          dump("yret%d" % l, AR[:, 2:4, :], [128, 2, T], BF16, AK(2, 0, T) + AK(3, 0, T))
          stage("ret")
          wout_partial(l, "ret", [2, 3])

          su = use_units(5)
          def qk_it(c, gi, lo, n):
              gcol = GQ8[:, l:l + 1] if c < 4 else vcol(l, "kg")
              b = ps_alloc()
              fm_proj(su[c], gi, b)
              s = sb_alloc()
              ACT(SB[:, s, 0:n], PS[:, b, 0:n], AF.Square, [("ps", b)], [("sb", s)])
              yield
              b2 = ps_alloc()
              MM(PS[:, b2, 0:n], BONES[:, :], SB[:, s, 0:n], True, True, [("sb", s), "bones"], [("ps", b2)])
              r = rstd_from_ps(b2, n, 1.0 / 64, 0)
              yield
              STT(A(c, lo, n), PS[:, b, 0:n], gcol, SF[:, r, 0:n], ALU.mult, ALU.mult,
                  [("ps", b), ("sf", r), "vec", "gq8"], AK(c, lo, n))
          run_pipe([qk_it(c, gi, lo, n) for c in range(5) for gi, (lo, n) in enumerate(GRP)], 3)
          dump("qk%d" % l, AR[:, 0:5, :], [128, 5, T], BF16, [k for c in range(5) for k in AK(c, 0, T)])
          ps_s = ring([0, 1, 2, 3])
          ps_nd = ring([4, 5, 6, 7])
          def att_it(t):
              if t == 0:
                  chunks = [("old", 4, 0)]
              elif t == 1:
                  chunks = [("old", 3, 0), ("new", 0, 1)]
              else:
                  chunks = [("new", 2, 0), ("new", 1, t - 1), ("new", 0, t)]
              bN, bD = ps_nd(), ps_nd()
              pts = {0: [], 1: []}
              for g in range(2):
                  pb = 64 * g
                  for (mode, ty, kt) in chunks:
                      bS = ps_s()
                      MM(PS[:, bS, 0:512].rearrange("p (a b) -> p a b", b=128),
                         AR[pb:pb + 64, 4, kt * 128:(kt + 1) * 128],
                         AR[pb:pb + 64, 0:4, t * 128:(t + 1) * 128], True, mode == "old",
                         AK(4, kt * 128, 128) + [k for c in range(4) for k in AK(c, t * 128, 128)], [("ps", bS)])
                      s = sb_alloc()
                      if mode == "old":
                          i = sf_alloc()
                          for c in range(4):
                              STT(SF[:, i, c * 128:(c + 1) * 128], CST[:, C_ATTD + ty * 128:C_ATTD + (ty + 1) * 128],
                                  -SLOPES[4 * g + c], PS[:, bS, c * 128:(c + 1) * 128], ALU.mult, ALU.add,
                                  [("ps", bS), "cst"], [("sf", i)])
                          ACT(SB[:, s, 0:512], SF[:, i, 0:512], AF.Exp, [("sf", i)], [("sb", s)])
                      else:
                          MM(PS[:, bS, 0:512], ALB[pb:pb + 2, A_A:A_A + 128], ALB[pb:pb + 2, ty * 512:(ty + 1) * 512],
                             False, True, ["alb"], [("ps", bS)])
                          ACT(SB[:, s, 0:512], PS[:, bS, 0:512], AF.Exp, [("ps", bS)], [("sb", s)])
                          if ty in (0, 1):
                              mcol = A_MC if ty == 0 else A_MP
                              TT("dve" if ty == 0 else "pool", SB[:, s, 0:512].rearrange("p (a b) -> p a b", b=128),
                                 SB[:, s, 0:512].rearrange("p (a b) -> p a b", b=128),
                                 ALB[:, mcol:mcol + 128].unsqueeze(1).to_broadcast([128, 4, 128]), ALU.mult,
                                 [("sb", s), "alb"], [("sb", s)])
                      pts[g].append((s, kt, mode, ty))
              yield
              for g in range(2):
                  pb = 64 * g
                  for ci, (s, kt, mode, ty) in enumerate(pts[g]):
                      MM(PS[pb:pb + 64, bN, 0:512], AR[:, 9, kt * 128 + pb:kt * 128 + pb + 64], SB[:, s, 0:512],
                         ci == 0, ci == len(pts[g]) - 1, AK(9, kt * 128, 128) + [("sb", s)], [("ps", bN)])
                  for ci, (s, kt, mode, ty) in enumerate(pts[g]):
                      ones_ap = ALB[:, A_OM:A_OM + 64] if (mode == "new" and ty == 2) else ONES[:, 0:64]
                      MM(PS[pb:pb + 64, bD, 0:512], ones_ap, SB[:, s, 0:512],
                         ci == 0, ci == len(pts[g]) - 1, ["ones", "alb", ("sb", s)], [("ps", bD)])
              yield
              i = sf_alloc()
              TT("dve", SF[:, i, 0:512].rearrange("p (a b) -> p a b", b=128),
                 PS[:, bD, 0:512].rearrange("p (a b) -> p a b", b=128),
                 ESK[:, 4 * l:4 * l + 4].unsqueeze(2).to_broadcast([128, 4, 128]), ALU.add,
                 [("ps", bD), "esk"], [("sf", i)])
              ACT(SF[:, i, 0:512], SF[:, i, 0:512], AF.Ln, [("sf", i)], [("sf", i)])
              yield
              ACT(SF[:, i, 0:512], SF[:, i, 0:512], AF.Exp, [("sf", i)], [("sf", i)], scale=-1.0)
              TT("dve", AR[:, 5:9, t * 128:(t + 1) * 128], PS[:, bN, 0:512].rearrange("p (a b) -> p a b", b=128),
                 SF[:, i, 0:512].rearrange("p (a b) -> p a b", b=128), ALU.mult,
                 [("ps", bN), ("sf", i)], [k for c in range(5, 9) for k in AK(c, t * 128, 128)])
          run_pipe([att_it(t) for t in range(NT)], 2)
          dump("yattraw%d" % l, AR[:, 5:9, :], [128, 4, T], BF16, [k for c in range(5, 9) for k in AK(c, 0, T)])
          norm_full(l, [5, 6, 7, 8], "aog", 512)
          wout_partial(l, "att", [5, 6, 7, 8], group_outer=True)
          dump("hmid%d" % l, hT[:, :, :], [128, 8, T], F32, [HK(c, gi) for c in range(8) for gi in range(5)])
          stage("att")

          rmsnorm_main(l, "ffn_g")
          ps_up = ring([0, 1, 2, 3, 4, 5])
          ps_dn = ring([6, 7])
          for pi, part in enumerate(PARTS):
              ffn_st = {}

              def ffn_it(ci, c, gi, lo, n):
                  if gi == 0:
                      ffn_st[c] = {"w": use_units(2), "nxt": None}
                  st = ffn_st[c]
                  sg, su_ = st["w"]
                  bg, bu = ps_up(), ps_up()
                  fm_proj(sg, gi, bg, GRP16)
                  fm_proj(su_, gi, bu, GRP16)
                  if gi == 0:
                      cur = [sb_alloc(), sb_alloc()]
                      for hu in range(2):
                          ACT(SB[:, cur[hu], 0:2], EPS[:, 0:2], AF.Copy, ["eps"], [("sbh", cur[hu])], scale=0.0)
                  else:
                      cur = st["nxt"]
                  nxt = [sb_alloc(), sb_alloc()] if gi < len(GRP16) - 1 else None
                  st["nxt"] = nxt
                  accs = []
                  for hu, b in ((0, bg), (1, bu)):
                      u = 2 * c + hu
                      wo = l * NV + VOFF["fw"] + u * 3
                      sbf = cur[hu]
                      ACT(SB[:, sbf, 2:2 + n], PS[:, b, 0:n], AF.Copy, [("ps", b)], [("sb", sbf)])
                      if nxt is not None:
                          ACT(SB[:, nxt[hu], 0:2], PS[:, b, n - 2:n], AF.Copy, [("ps", b)], [("sbh", nxt[hu])])
                      i = sf_alloc()
                      ACT(SF[:, i, 0:n], PS[:, b, 0:n], AF.Identity, [("ps", b), "vec"], [("sf", i)],
                          scale=VEC[:, wo + 2:wo + 3], bias=vcol(l, "fb", u))
                      accs.append(i)
                  yield
                  for hu in range(2):
                      u = 2 * c + hu
                      wo = l * NV + VOFF["fw"] + u * 3
                      sbf, i = cur[hu], accs[hu]
                      STT(SF[:, i, 0:n], SB[:, sbf, 1:1 + n], VEC[:, wo + 1:wo + 2], SF[:, i, 0:n], ALU.mult, ALU.add,
                          [("sb", sbf), ("sbh", sbf), ("sf", i), "vec"], [("sf", i)])
                      STT(SF[:, i, 0:n], SB[:, sbf, 0:n], VEC[:, wo:wo + 1], SF[:, i, 0:n], ALU.mult, ALU.add,
                          [("sb", sbf), ("sbh", sbf), ("sf", i), "vec"], [("sf", i)])
                  yield
                  ig, iu = accs
                  ACT(SF[:, ig, 0:n], SF[:, ig, 0:n], AF.Silu, [("sf", ig)], [("sf", ig)])
                  yield
                  TT("pool", A(ci, lo, n), SF[:, ig, 0:n], SF[:, iu, 0:n], ALU.mult, [("sf", ig), ("sf", iu)],
                     AK(ci, lo, n))
              run_pipe([ffn_it(ci, c, gi, lo, n) for ci, c in enumerate(part) for gi, (lo, n) in enumerate(GRP16)], 3)
              if pi == 0:
                  dump("g%d" % l, AR[:, 0:8, :], [128, 8, T], BF16, [k for c in range(8) for k in AK(c, 0, T)])
              npc = len(part)

              def down_one(s, m, gi, lo, n):
                  b = ps_dn()
                  for ci in range(npc):
                      MM(PS[:, b, 0:n], W8[:, s, ci * 128:(ci + 1) * 128], A(ci, lo, n), ci == 0, ci == npc - 1,
                         [("w", s)] + AK(ci, lo, n), [("ps", b)])
                  TT("dve", hT[:, m, lo:lo + n], hT[:, m, lo:lo + n], PS[:, b, 0:n], ALU.add,
                     [HK(m, gi), ("ps", b)], [HK(m, gi)])
              if pi == len(PARTS) - 1 and not last:
                  for hf in range(2):
                      su4 = use_units(4)
                      for gi, (lo, n) in enumerate(GRP16):
                          for mi in range(4):
                              down_one(su4[mi], hf * 4 + mi, gi, lo, n)
              else:
                  for m in range(8):
                      (s,) = use_units(1)
                      for gi, (lo, n) in enumerate(GRP16):
                          down_one(s, m, gi, lo, n)
                      if last and pi == len(PARTS) - 1:
                          out_ops.append(DMA("sp", outT[m * 128:(m + 1) * 128, :], hT[:, m, 128:T],
                                             [HK(m, gi) for gi in range(1, 5)], (), ("out", m)))
          dump("hout%d" % l, hT[:, :, :], [128, 8, T], F32, [HK(c, gi) for c in range(8) for gi in range(5)])
    except _Stop:
        out_ops = [DMA("sp", outT[m * 128:(m + 1) * 128, :], hT[:, m, 128:T],
                       [HK(m, gi) for gi in range(1, 5)], (), ("out", m)) for m in range(8)]
    P.emit(final_wait_ops=out_ops + cx.dump_outs)
    return nc


_CACHE = {}


def make_in_maps(inp):
    x = np.asarray(inp["x"], np.float32)
    wts = pack_units(np.asarray(inp["w_in"], np.float32), np.asarray(inp["cv_pw"], np.float32),
                     np.asarray(inp["w_out"], np.float32), np.asarray(inp["ffn_up"], np.float32),
                     np.asarray(inp["ffn_down"], np.float32))
    vec = pack_vecs({k: np.asarray(v, np.float32) for k, v in inp.items()})
    cst = const_tables()
    alb = alibi_tables()
    metaT = np.ascontiguousarray(np.asarray(inp["meta"], np.float32).T)
    snk = np.ascontiguousarray(np.asarray(inp["attn_sinks"], np.float32))
    maps = []
    for b in range(8):
        maps.append({"xT": np.ascontiguousarray(x[b].T), "metaT": metaT, "wts": wts, "vec": vec,
                     "cst": cst, "snk": snk, "alb": alb})
    return maps


def kernel(**inputs):
    if "nc" not in _CACHE:
        _CACHE["nc"] = build_program()
    nc = _CACHE["nc"]
    maps = make_in_maps(inputs)
    res = run_bass_kernel_spmd(nc, maps, core_ids=list(range(8)))
    out = np.stack([np.ascontiguousarray(r["outT"].T) for r in res.results], axis=0)
    return out.astype(np.float32)
```
